# Optimizing a Trainium2 kernel written in Bass

```python
import jax, jax.numpy as jnp
from jax import lax
import numpy as np

D_MODEL = 1024
BATCH = 16
SEQ = 2048
DEPTH = 2

N_MIXERS = 2
N_HGRN_LAYERS = (DEPTH + N_MIXERS - 1) // N_MIXERS
N_CONV_LAYERS = DEPTH // N_MIXERS
MEM_LEN = 256
HGRN_HEADS = 8
HGRN_DIM = D_MODEL // HGRN_HEADS
HGRN_CHUNK = 64
CONV_WIDTH = 31
XATTN_HEADS = 4
XATTN_DIM = D_MODEL // XATTN_HEADS
D_FF = 2816
RMS_EPS = 1e-6
LN_EPS = 1e-5

kernel_name = "hgrn2_conformer_conv_interleaved_macaron_memxattn"


def rms_norm(x, g):
    xf = x.astype(jnp.float32)
    y = xf * lax.rsqrt(jnp.mean(xf * xf, axis=-1, keepdims=True) + RMS_EPS)
    return (y * g.astype(jnp.float32)).astype(x.dtype)


def layer_norm(x, g, b):
    xf = x.astype(jnp.float32)
    mu = jnp.mean(xf, axis=-1, keepdims=True)
    var = jnp.mean(jnp.square(xf - mu), axis=-1, keepdims=True)
    y = (xf - mu) * lax.rsqrt(var + LN_EPS)
    return (y * g.astype(jnp.float32) + b.astype(jnp.float32)).astype(x.dtype)


def swiglu_ffn(h, w_in, w_out):
    gate, up = jnp.split(h @ w_in, 2, axis=-1)
    return (jax.nn.silu(gate) * up) @ w_out


def hgrn2_mixer(h, w_in, head_norm, w_out, lb):
    B, S, D = h.shape
    n_chunks = S // HGRN_CHUNK
    f32 = jnp.float32
    q, f, i, g = jnp.split(h @ w_in, 4, axis=-1)
    q = jax.nn.silu(q.astype(f32))
    ff = f.astype(f32)
    lbf = lb.astype(f32)
    log_forget = jnp.log(lbf + (1.0 - lbf) * jax.nn.sigmoid(ff))
    k = (1.0 - lbf) * jax.nn.sigmoid(-ff)
    v = i.astype(f32)

    def to_chunks(t):
        return t.reshape(B, n_chunks, HGRN_CHUNK, HGRN_HEADS, HGRN_DIM).transpose(1, 0, 3, 2, 4)

    causal = jnp.tril(jnp.ones((HGRN_CHUNK, HGRN_CHUNK), dtype=bool))

    def chunk_step(state, inp):
        qb, kb, vb, gb = inp
        G = jnp.cumsum(gb, axis=2)
        o_inter = jnp.einsum('bhtk,bhkv->bhtv', qb * jnp.exp(G), state)
        diff = G[:, :, :, None, :] - G[:, :, None, :, :]
        decay = jnp.exp(jnp.where(causal[None, None, :, :, None], diff, -jnp.inf))
        scores = jnp.einsum('bhtk,bhsk,bhtsk->bhts', qb, kb, decay)
        o_intra = jnp.einsum('bhts,bhsv->bhtv', scores, vb)
        G_last = G[:, :, -1:, :]
        k_dec = kb * jnp.exp(G_last - G)
        new_state = state * jnp.exp(G_last[:, :, 0, :, None]) + jnp.einsum('bhsk,bhsv->bhkv', k_dec, vb)
        return new_state, o_inter + o_intra

    state0 = jnp.zeros((B, HGRN_HEADS, HGRN_DIM, HGRN_DIM), f32)
    _, o = lax.scan(chunk_step, state0, (to_chunks(q), to_chunks(k), to_chunks(v), to_chunks(log_forget)))
    o = o.transpose(1, 0, 3, 2, 4).reshape(B, S, HGRN_HEADS, HGRN_DIM).astype(h.dtype)
    o = rms_norm(o, head_norm.reshape(HGRN_HEADS, HGRN_DIM)).reshape(B, S, D)
    o = o * jax.nn.silu(g)
    return o @ w_out


def conformer_conv_mixer(h, w_in, b_in, dw, dw_b, ln_g, ln_b, w_out, b_out):
    D = h.shape[-1]
    a, gate = jnp.split(h @ w_in + b_in, 2, axis=-1)
    u = a * jax.nn.sigmoid(gate)
    u = lax.conv_general_dilated(
        u, dw.astype(u.dtype)[:, None, :], window_strides=(1,),
        padding=[(CONV_WIDTH - 1, 0)],
        dimension_numbers=('NWC', 'WIO', 'NWC'), feature_group_count=D) + dw_b
    u = jax.nn.silu(layer_norm(u, ln_g, ln_b))
    return u @ w_out + b_out


def memory_cross_attention(h, mem_n, wq, wkv, wo):
    B, S, D = h.shape
    q = (h @ wq).reshape(B, S, XATTN_HEADS, XATTN_DIM)
    k, v = jnp.split(mem_n @ wkv, 2, axis=-1)
    k = k.reshape(B, MEM_LEN, XATTN_HEADS, XATTN_DIM)
    v = v.reshape(B, MEM_LEN, XATTN_HEADS, XATTN_DIM)
    s = jnp.einsum('bshd,bmhd->bhsm', q, k).astype(jnp.float32) * (XATTN_DIM ** -0.5)
    p = jax.nn.softmax(s, axis=-1).astype(v.dtype)
    o = jnp.einsum('bhsm,bmhd->bshd', p, v).reshape(B, S, D)
    return o @ wo


def setup_inputs(seed: int = 0) -> dict:
    key = jax.random.key(seed)
    ks = iter(jax.random.split(key, 40))
    D, F, L = D_MODEL, D_FF, DEPTH
    nA, nB = N_HGRN_LAYERS, N_CONV_LAYERS

    def w(shape, fan_in):
        return jax.random.normal(next(ks), shape, jnp.float32) * (fan_in ** -0.5)

    def gain(shape):
        return 1.0 + 0.02 * jax.random.normal(next(ks), shape, jnp.float32)

    def bias(shape):
        return 0.02 * jax.random.normal(next(ks), shape, jnp.float32)

    return {
        "x": jax.random.normal(next(ks), (BATCH, SEQ, D), jnp.float32),
        "mem": jax.random.normal(next(ks), (BATCH, MEM_LEN, D), jnp.float32),
        "ffn1_norm": gain((L, D)),
        "ffn1_w_in": w((L, D, 2 * F), D),
        "ffn1_w_out": w((L, F, D), F),
        "mix_norm": gain((L, D)),
        "hgrn_w_in": w((nA, D, 4 * D), D),
        "hgrn_head_norm": gain((nA, D)),
        "hgrn_w_out": w((nA, D, D), D),
        "hgrn_lb_logits": 0.1 * jax.random.normal(next(ks), (nA + 1, D), jnp.float32),
        "conv_w_in": w((nB, D, 2 * D), D),
        "conv_b_in": bias((nB, 2 * D)),
        "conv_dw": w((nB, CONV_WIDTH, D), CONV_WIDTH),
        "conv_dw_b": bias((nB, D)),
        "conv_ln_g": gain((nB, D)),
        "conv_ln_b": bias((nB, D)),
        "conv_w_out": w((nB, D, D), D),
        "conv_b_out": bias((nB, D)),
        "xattn_norm": gain((L, D)),
        "xattn_wq": w((L, D, D), D),
        "xattn_wkv": w((L, D, 2 * D), D),
        "xattn_wo": w((L, D, D), D),
        "ffn2_norm": gain((L, D)),
        "ffn2_w_in": w((L, D, 2 * F), D),
        "ffn2_w_out": w((L, F, D), F),
        "mem_norm": gain((D,)),
        "final_norm": gain((D,)),
    }


def reference(x, mem, ffn1_norm, ffn1_w_in, ffn1_w_out, mix_norm,
              hgrn_w_in, hgrn_head_norm, hgrn_w_out, hgrn_lb_logits,
              conv_w_in, conv_b_in, conv_dw, conv_dw_b, conv_ln_g, conv_ln_b, conv_w_out, conv_b_out,
              xattn_norm, xattn_wq, xattn_wkv, xattn_wo,
              ffn2_norm, ffn2_w_in, ffn2_w_out, mem_norm, final_norm):
    mem_n = rms_norm(mem, mem_norm)
    lower_bounds = jnp.cumsum(jax.nn.softmax(hgrn_lb_logits.astype(jnp.float32), axis=0), axis=0)
    h = x
    for layer in range(DEPTH):
        h = h + 0.5 * swiglu_ffn(rms_norm(h, ffn1_norm[layer]), ffn1_w_in[layer], ffn1_w_out[layer])
        hn = rms_norm(h, mix_norm[layer])
        j = layer // N_MIXERS
        if layer % N_MIXERS == 0:
            mixed = hgrn2_mixer(hn, hgrn_w_in[j], hgrn_head_norm[j], hgrn_w_out[j], lower_bounds[j])
        else:
            mixed = conformer_conv_mixer(hn, conv_w_in[j], conv_b_in[j], conv_dw[j], conv_dw_b[j],
                                         conv_ln_g[j], conv_ln_b[j], conv_w_out[j], conv_b_out[j])
        h = h + mixed.astype(h.dtype)
        h = h + memory_cross_attention(rms_norm(h, xattn_norm[layer]), mem_n,
                                       xattn_wq[layer], xattn_wkv[layer], xattn_wo[layer])
        h = h + 0.5 * swiglu_ffn(rms_norm(h, ffn2_norm[layer]), ffn2_w_in[layer], ffn2_w_out[layer])
    return rms_norm(h, final_norm)
```

```python
import numpy as np
import concourse.bass as bass
import concourse.mybir as mybir
from concourse.bass_utils import run_bass_kernel_spmd

F32 = mybir.dt.float32
BF16 = mybir.dt.bfloat16
AF = mybir.ActivationFunctionType
ALU = mybir.AluOpType

D = 1024
KC = 8
FF = 2816
FC = 22
MEM = 256
CW = 31
PAD = 32
N_CORES = 8
RMS_EPS = 1e-6
LN_EPS = 1e-5
PG = 256

ENGS = ("pe", "act", "dve", "pool", "sp")


class Prog:
    def __init__(self):
        self.ops = []
        self.last_w = {}
        self.readers = {}
        self.bank_rr = 0
        self.pinned = set()

    def bank(self, pin=False):
        while True:
            b = self.bank_rr % 8
            self.bank_rr += 1
            if b not in self.pinned:
                break
        if pin:
            self.pinned.add(b)
        return b

    def unpin(self, b):
        self.pinned.discard(b)

    def op(self, eng, fn, reads=(), writes=(), dma=False):
        idx = len(self.ops)
        deps = {}

        def add(i):
            p = self.ops[i]
            if p["dma"]:
                deps[("d", i)] = i
            else:
                e = p["eng"]
                if deps.get(e, -1) < i:
                    deps[e] = i
        for k in reads:
            w = self.last_w.get(k)
            if w is not None:
                add(w)
        for k in writes:
            w = self.last_w.get(k)
            if w is not None:
                add(w)
            for r in self.readers.get(k, {}).values():
                add(r)
        o = dict(eng=eng, fn=fn, deps=list(deps.values()), dma=dma, signal=False)
        self.ops.append(o)
        for k in reads:
            self.readers.setdefault(k, {})[("d", idx) if dma else eng] = idx
        for k in writes:
            self.last_w[k] = idx
            self.readers[k] = {}
        return idx

    def emit(self, sems, dma_sems):
        ops = self.ops
        for o in ops:
            for d in o["deps"]:
                p = ops[d]
                if p["eng"] == "pe" and o["eng"] == "pe" and not p["dma"] and not o["dma"]:
                    continue
                p["signal"] = True
        cnt = {e: 0 for e in ENGS}
        slot_cnt = [0] * len(dma_sems)
        half = len(dma_sems) // 2
        slot_rr = {"sp": 0, "pool": 0}
        for o in ops:
            if o["dma"]:
                s = slot_rr[o["eng"]] % half + (0 if o["eng"] == "sp" else half)
                slot_rr[o["eng"]] += 1
                o["prev"] = slot_cnt[s]
                slot_cnt[s] += 16
                o["sem"] = ("dma", s)
                o["val"] = slot_cnt[s]
            elif o["signal"]:
                cnt[o["eng"]] += 1
                o["sem"] = ("eng", o["eng"])
                o["val"] = cnt[o["eng"]]
        per_eng = {e: [o for o in ops if o["eng"] == e] for e in ENGS}
        self.stats = {e: len(per_eng[e]) for e in ENGS}
        self.stats["signals"] = dict(cnt)

        def semobj(s):
            return dma_sems[s[1]] if s[0] == "dma" else sems[s[1]]

        def run(eng_name, eng):
            waited = {}
            for o in per_eng[eng_name]:
                need = {}
                for d in o["deps"]:
                    p = ops[d]
                    if p["eng"] == "pe" and eng_name == "pe" and not p["dma"] and not o["dma"]:
                        continue
                    s = p["sem"]
                    if need.get(s, 0) < p["val"]:
                        need[s] = p["val"]
                if o["dma"] and o["prev"] > 0:
                    s = o["sem"]
                    if need.get(s, 0) < o["prev"]:
                        need[s] = o["prev"]
                for s, v in need.items():
                    if waited.get(s, 0) >= v:
                        continue
                    eng.wait_ge(semobj(s), v)
                    waited[s] = v
                ins = o["fn"](eng)
                if o["dma"]:
                    ins.then_inc(semobj(o["sem"]), 16)
                elif o["signal"]:
                    ins.then_inc(semobj(o["sem"]), 1)
            for o in per_eng[eng_name]:
                if o["dma"]:
                    s = o["sem"]
                    if waited.get(s, 0) < o["val"]:
                        eng.wait_ge(semobj(s), o["val"])
                        waited[s] = o["val"]
        return run


class ABuf:
    def __init__(self, arena, off_bytes, n, dtype):
        esz = 4 if dtype == F32 else 2
        assert off_bytes % 4 == 0
        nb = n * esz
        assert nb % 4 == 0
        self.off = off_bytes
        self.esz = esz
        self.n = n
        a = arena[:, off_bytes // 4:(off_bytes + nb) // 4]
        self.ap = a if dtype == F32 else a.bitcast(BF16)
        assert tuple(self.ap.shape) == (128, n), (self.ap.shape, n)

    def keys(self, lo=0, hi=None):
        hi = self.n if hi is None else hi
        b0 = self.off + lo * self.esz
        b1 = self.off + hi * self.esz
        return [("pg", i) for i in range(b0 // PG, (b1 - 1) // PG + 1)]


class Carver:
    def __init__(self, arena, nbytes):
        self.arena = arena
        self.nbytes = nbytes
        self.off = 0

    base = 0

    def reset(self):
        self.off = self.base

    def take(self, n, dtype):
        esz = 4 if dtype == F32 else 2
        nb = (n * esz + PG - 1) // PG * PG
        assert self.off + nb <= self.nbytes, ("arena overflow", self.off, nb, self.nbytes)
        b = ABuf(self.arena, self.off, n, dtype)
        self.off += nb
        return b


def build(T=2048, NSEQ=2, stages=None, dump=None):
    NT = T // 128
    NB = T // 512
    ALL = ["ffn1_0", "mix_0", "xattn_0", "ffn2_0", "ffn1_1", "mix_1", "xattn_1", "ffn2_1", "final"]
    stages = ALL if stages is None else stages

    nc = bass.Bass("TRN2", target_bir_lowering=False)

    def din(name, shape):
        return nc.dram_tensor(name, list(shape), F32, kind="ExternalInput").ap()

    x_d = din("x", [NSEQ, T, D])
    mem_d = din("mem", [NSEQ, MEM, D])
    ffn_norm_d = [din("ffn1_norm", [2, D]), din("ffn2_norm", [2, D])]
    ffn_win_d = [din("ffn1_w_in", [2, D, 2 * FF]), din("ffn2_w_in", [2, D, 2 * FF])]
    ffn_wout_d = [din("ffn1_w_out", [2, FF, D]), din("ffn2_w_out", [2, FF, D])]
    mix_norm_d = din("mix_norm", [2, D])
    hg_win_d = din("hgrn_w_in", [1, D, 4 * D])
    hg_hn_d = din("hgrn_head_norm", [1, D])
    hg_wout_d = din("hgrn_w_out", [1, D, D])
    hg_lb_d = din("hgrn_lb_logits", [2, D])
    cv_win_d = din("conv_w_in", [1, D, 2 * D])
    cv_bin_d = din("conv_b_in", [1, 2 * D])
    cv_dw_d = din("conv_dw", [1, CW, D])
    cv_dwb_d = din("conv_dw_b", [1, D])
    cv_lng_d = din("conv_ln_g", [1, D])
    cv_lnb_d = din("conv_ln_b", [1, D])
    cv_wout_d = din("conv_w_out", [1, D, D])
    cv_bout_d = din("conv_b_out", [1, D])
    xa_norm_d = din("xattn_norm", [2, D])
    xa_wq_d = din("xattn_wq", [2, D, D])
    xa_wkv_d = din("xattn_wkv", [2, D, 2 * D])
    xa_wo_d = din("xattn_wo", [2, D, D])
    mem_norm_d = din("mem_norm", [D])
    fin_norm_d = din("final_norm", [D])
    y_d = nc.dram_tensor("y", [NSEQ, T, D], F32, kind="ExternalOutput").ap()
    dbg_d = nc.dram_tensor("dbg", [128, 8, 512], F32, kind="ExternalOutput").ap() if dump is not None else None

    HNT_B = KC * T * 2
    ARENA_B = HNT_B + 84 * 1024
    from contextlib import ExitStack
    es = ExitStack()
    with es:
        def sb(name, shape, dt):
            return es.enter_context(nc.sbuf_tensor(name, list(shape), dt))

        h = sb("h", [128, NT, D], F32)
        arena = sb("arena", [128, ARENA_B // 4], F32)
        gB = sb("gB", [128, 2, D], F32)
        hn_tok = sb("hn_tok", [128, 2, D], BF16)
        junk = sb("junk", [128, 2, D], BF16)
        ident = sb("ident", [128, 128], BF16)
        identf = sb("identf", [128, 128], F32)
        ones = sb("ones", [128, 128], BF16)
        onesf = sb("onesf", [128, 128], F32)
        maskT = sb("maskT", [128, 128], BF16)
        rmask = sb("rmask", [128, 512], F32)
        stat = sb("stat", [128, 3, 32], F32)
        vecs = sb("vecs", [128, 16, 8], F32)
        dwT = sb("dwT", [128, KC, CW], F32)
        memnT = sb("memnT", [128, KC, MEM], BF16)
        stf = sb("stf", [128, 128], F32)
        stb = sb("stb", [128, 128], BF16)
        sttmp = sb("sttmp", [128, 128], F32)
        ps = es.enter_context(nc.psum_tensor("ps", [128, 8, 512], F32))
        sems = {e: es.enter_context(nc.semaphore("s_" + e)) for e in ENGS}
        dma_sems = [es.enter_context(nc.semaphore("dq%d" % i)) for i in range(24)]
        block = es.enter_context(nc.Block())

        P = Prog()
        CV = Carver(arena, ARENA_B)
        CV.base = HNT_B
        hnT_buf = ABuf(arena, 0, KC * T, BF16)
        hnT = hnT_buf.ap.rearrange("p (c t) -> p c t", t=T)
        memt_buf = ABuf(arena, ARENA_B - 8192, 2 * D, F32)
        memt = memt_buf.ap.rearrange("p (t d) -> p t d", d=D)
        dwtok_buf = ABuf(arena, ARENA_B - 8192 - 4096, D, F32)
        dwtok = dwtok_buf.ap[0:CW, :]
        V_LB, V_OML, V_NOML, V_HN, V_BA, V_BG, V_DWB, V_LNG, V_LNB, V_L0, V_L1 = range(11)

        def psk(b):
            return [("ps", b)]

        def psb(b):
            return ps[:, b, :]

        def psb16(b):
            return ps[:, b, :].bitcast(BF16)

        for blk in range(NB):
            P.op("sp", lambda e, blk=blk: e.dma_start(out=h[:, blk * 4:(blk + 1) * 4, :], in_=x_d[0, blk * 512:(blk + 1) * 512, :].rearrange("(t p) d -> p t d", p=128)),
                 writes=[("h", blk * 4 + i) for i in range(4)], dma=True)
        P.op("pool", lambda e: e.memset(identf[:], 0.0), writes=["identf"])
        P.op("pool", lambda e: e.affine_select(out=identf[:], in_=identf[:], pattern=[[-1, 128]], compare_op=ALU.not_equal, fill=1.0, base=0, channel_multiplier=1), reads=["identf"], writes=["identf"])
        P.op("pool", lambda e: e.tensor_copy(out=ident[:], in_=identf[:]), reads=["identf"], writes=["ident"])
        P.op("pool", lambda e: e.memset(onesf[:], 1.0), writes=["onesf"])
        P.op("pool", lambda e: e.tensor_copy(out=ones[:], in_=onesf[:]), reads=["onesf"], writes=["ones"])
        P.op("pool", lambda e: e.affine_select(out=onesf[:], in_=onesf[:], pattern=[[1, 128]], compare_op=ALU.is_ge, fill=0.0, base=0, channel_multiplier=-1), reads=["onesf"], writes=["onesf"])
        P.op("pool", lambda e: e.tensor_copy(out=maskT[:], in_=onesf[:]), reads=["onesf"], writes=["maskT"])
        P.op("pool", lambda e: e.memset(rmask[:], 1.0), writes=["rmask"])
        P.op("pool", lambda e: e.memset(rmask[:].rearrange("p (c t) -> p c t", t=128)[:, :, 0:1], 0.0), reads=["rmask"], writes=["rmask"])

        P.op("pool", lambda e: e.memset(stf[:], 0.0), writes=["stf"])
        P.op("pool", lambda e: e.memset(stb[:], 0.0), writes=["stb"])
        def load_vec(slot, src):
            P.op("sp", lambda e: e.dma_start(out=vecs[:, slot, :], in_=src.rearrange("(c p) -> p c", p=128), allow_slow_non_contiguous=True), writes=[("vec", slot)], dma=True)
        load_vec(V_L0, hg_lb_d[0])
        load_vec(V_L1, hg_lb_d[1])
        load_vec(V_HN, hg_hn_d[0])
        load_vec(V_BA, cv_bin_d[0, 0:D])
        load_vec(V_BG, cv_bin_d[0, D:2 * D])
        load_vec(V_DWB, cv_dwb_d[0])
        load_vec(V_LNG, cv_lng_d[0])
        load_vec(V_LNB, cv_lnb_d[0])
        P.op("dve", lambda e: e.tensor_tensor(out=vecs[:, V_LB, :], in0=vecs[:, V_L0, :], in1=vecs[:, V_L1, :], op=ALU.subtract), reads=[("vec", V_L0), ("vec", V_L1)], writes=[("vec", V_LB)])
        P.op("act", lambda e: e.activation(out=vecs[:, V_LB, :], in_=vecs[:, V_LB, :], func=AF.Sigmoid), reads=[("vec", V_LB)], writes=[("vec", V_LB)])
        P.op("dve", lambda e: e.tensor_scalar(out=vecs[:, V_OML, :], in0=vecs[:, V_LB, :], scalar1=-1.0, scalar2=1.0, op0=ALU.mult, op1=ALU.add), reads=[("vec", V_LB)], writes=[("vec", V_OML)])
        P.op("dve", lambda e: e.tensor_scalar(out=vecs[:, V_NOML, :], in0=vecs[:, V_LB, :], scalar1=1.0, scalar2=-1.0, op0=ALU.mult, op1=ALU.add), reads=[("vec", V_LB)], writes=[("vec", V_NOML)])
        P.op("sp", lambda e: e.dma_start(out=dwtok[:], in_=cv_dw_d[0]), writes=dwtok_buf.keys(), dma=True)
        b0 = P.bank()
        for c in range(KC):
            P.op("pe", lambda e, c=c: e.transpose(out=ps[:, b0, c * 32:c * 32 + CW], in_=dwtok[:, c * 128:(c + 1) * 128], identity=identf[0:CW, 0:CW]), reads=dwtok_buf.keys() + ["identf"], writes=psk(b0))
        P.op("dve", lambda e: e.tensor_copy(out=dwT[:], in_=ps[:, b0, 0:256].rearrange("p (c j) -> p c j", j=32)[:, :, 0:CW]), reads=psk(b0), writes=["dwT"])

        def load_gain(slot, src):
            P.op("sp", lambda e: e.dma_start(out=gB[:, slot, :], in_=src.partition_broadcast(128)), writes=[("gB", slot)], dma=True)

        gslot = [0]

        def norm_stats(src_tile_ap, col, rkeys):
            js = col % 2
            P.op("act", lambda e: e.activation(out=junk[:, js, :], in_=src_tile_ap, func=AF.Square, accum_out=stat[:, 0, col:col + 1]), reads=rkeys, writes=[("ss", col), ("junk", js)])

        def norm_rstd(n):
            P.op("act", lambda e: e.activation(out=stat[:, 1, 0:n], in_=stat[:, 0, 0:n], func=AF.Sqrt, bias=RMS_EPS, scale=1.0 / D), reads=[("ss", c) for c in range(n)], writes=["sd"])
            P.op("dve", lambda e: e.reciprocal(out=stat[:, 2, 0:n], in_=stat[:, 1, 0:n]), reads=["sd"], writes=["rstd"])

        def begin_norm(gain_src):
            g = gslot[0] % 2
            gslot[0] += 1
            load_gain(g, gain_src)
            return g

        def block_rstd(blk):
            c0 = blk * 4
            for tt in range(c0, c0 + 4):
                norm_stats(h[:, tt, :], tt, [("h", tt)])
            P.op("act", lambda e: e.activation(out=stat[:, 1, c0:c0 + 4], in_=stat[:, 0, c0:c0 + 4], func=AF.Sqrt, bias=RMS_EPS, scale=1.0 / D), reads=[("ss", c) for c in range(c0, c0 + 4)], writes=[("sd", blk)])
            P.op("dve", lambda e: e.reciprocal(out=stat[:, 2, c0:c0 + 4], in_=stat[:, 1, c0:c0 + 4]), reads=[("sd", blk)], writes=[("rstd", blk)])

        def norm_block(blk, g, skip_rstd=False):
            if not skip_rstd:
                block_rstd(blk)
            for tt in range(blk * 4, blk * 4 + 4):
                s = tt % 2
                P.op("dve", lambda e, tt=tt, s=s: e.scalar_tensor_tensor(out=hn_tok[:, s, :], in0=h[:, tt, :], scalar=stat[:, 2, tt:tt + 1], in1=gB[:, g, :], op0=ALU.mult, op1=ALU.mult),
                     reads=[("h", tt), ("rstd", blk), ("gB", g)], writes=[("hn_tok", s)])
                b = P.bank()
                for c in range(KC):
                    P.op("pe", lambda e, c=c, s=s, b=b: e.transpose(out=psb16(b)[:, c * 128:(c + 1) * 128], in_=hn_tok[:, s, c * 128:(c + 1) * 128], identity=ident[:]),
                         reads=[("hn_tok", s), "ident"], writes=psk(b))
                eng = "act" if tt % 2 == 0 else "dve"
                if eng == "act":
                    fn = lambda e, tt=tt, b=b: e.activation(out=hnT[:, :, tt * 128:(tt + 1) * 128], in_=psb16(b).rearrange("p (c t) -> p c t", t=128), func=AF.Copy)
                else:
                    fn = lambda e, tt=tt, b=b: e.tensor_copy(out=hnT[:, :, tt * 128:(tt + 1) * 128], in_=psb16(b).rearrange("p (c t) -> p c t", t=128))
                P.op(eng, fn, reads=psk(b), writes=[k for c in range(KC) for k in hnT_buf.keys(c * T + tt * 128, c * T + (tt + 1) * 128)])

        fin_stage = [ABuf(arena, ARENA_B - 16384 + i * 4096, D, F32) for i in range(2)]
        fin_i = [0]

        def final_block(blk, g, sq_i):
            block_rstd(blk)
            for tt in range(blk * 4, blk * 4 + 4):
                fs = fin_stage[fin_i[0] % 2]
                fin_i[0] += 1
                P.op("dve", lambda e, tt=tt, fs=fs: e.scalar_tensor_tensor(out=fs.ap, in0=h[:, tt, :], scalar=stat[:, 2, tt:tt + 1], in1=gB[:, g, :], op0=ALU.mult, op1=ALU.mult),
                     reads=[("h", tt), ("rstd", blk), ("gB", g)], writes=fs.keys())
                P.op("sp", lambda e, tt=tt, fs=fs: e.dma_start(out=y_d[sq_i, tt * 128:(tt + 1) * 128, :], in_=fs.ap), reads=fs.keys(), writes=[("y", sq_i, tt)], dma=True)

        def hnT_keys(blk):
            if blk not in hk_cache:
                hk_cache[blk] = [k for c in range(KC) for k in hnT_buf.keys(c * T + blk * 512, c * T + (blk + 1) * 512)]
            return hk_cache[blk]
        hk_cache = {}

        def wload(dst_ap, src_ap, wkeys):
            P.op("pool", lambda e: e.dma_start(out=dst_ap, in_=src_ap), writes=wkeys, dma=True)

        def add_to_h(tt, half, b):
            P.op("dve", lambda e: e.tensor_tensor(out=h[:, tt, half * 512:(half + 1) * 512], in0=psb(b), in1=h[:, tt, half * 512:(half + 1) * 512], op=ALU.add),
                 reads=psk(b) + [("h", tt)], writes=[("h", tt)])

        def emit_ffn(which, layer, hook):
            win = ffn_win_d[which][layer]
            wout = ffn_wout_d[which][layer]
            CV.reset()
            NS = 3
            wi = [CV.take(KC * 512, BF16) for _ in range(NS)]
            wo = [CV.take(2 * D, BF16) for _ in range(NS)]
            sg = [CV.take(512, BF16) for _ in range(2)]
            act = [CV.take(2 * 512, BF16) for _ in range(3)]
            NG = FC // 2

            def load_group(g):
                s = g % NS
                wiv = wi[s].ap.rearrange("p (a k n) -> p a k n", a=2, n=256)
                f0 = g * 256
                wload(wiv[:, 0], win[:, f0:f0 + 256].rearrange("(k p) n -> p k n", p=128), wi[s].keys(0, KC * 256))
                wload(wiv[:, 1], win[:, FF + f0:FF + f0 + 256].rearrange("(k p) n -> p k n", p=128), wi[s].keys(KC * 256, 2 * KC * 256))
                wload(wo[s].ap.rearrange("p (j n) -> p j n", n=D), wout[f0:f0 + 256, :].rearrange("(j p) n -> p j n", p=128), wo[s].keys())

            load_group(0)
            load_group(1)
            ai = [0]
            pending = [None]

            def gateup(g, blk, a):
                s = g % NS
                wiv = wi[s].ap.rearrange("p (a k n) -> p a k n", a=2, n=256)
                actv = act[a].ap.rearrange("p (j n) -> p j n", n=512)
                for j in range(2):
                    bg = P.bank()
                    bu = P.bank()
                    for (bb, ga) in ((bg, 0), (bu, 1)):
                        for k in range(KC):
                            P.op("pe", lambda e, bb=bb, ga=ga, k=k, j=j: e.matmul(psb(bb), lhsT=wiv[:, ga, k, j * 128:(j + 1) * 128], rhs=hnT[:, k, blk * 512:(blk + 1) * 512], start=(k == 0), stop=(k == KC - 1)),
                                 reads=wi[s].keys(ga * KC * 256, (ga + 1) * KC * 256) + hnT_keys(blk), writes=psk(bb))
                    P.op("act", lambda e, bg=bg, j=j: e.activation(out=sg[j].ap, in_=psb(bg), func=AF.Silu), reads=psk(bg), writes=sg[j].keys())
                    P.op("dve", lambda e, bu=bu, j=j: e.tensor_tensor(out=actv[:, j, :], in0=psb(bu), in1=sg[j].ap, op=ALU.mult),
                         reads=psk(bu) + sg[j].keys(), writes=act[a].keys(j * 512, (j + 1) * 512))

            def outproj(g, blk, a):
                s = g % NS
                wov = wo[s].ap.rearrange("p (j n) -> p j n", n=D)
                actv = act[a].ap.rearrange("p (j n) -> p j n", n=512)
                for t4 in range(4):
                    tt = blk * 4 + t4
                    for half in range(2):
                        bo = P.bank()
                        for j in range(2):
                            P.op("pe", lambda e, bo=bo, j=j, t4=t4, half=half: e.matmul(psb(bo), lhsT=actv[:, j, t4 * 128:(t4 + 1) * 128], rhs=wov[:, j, half * 512:(half + 1) * 512], start=(j == 0), stop=(j == 1)),
                                 reads=act[a].keys() + wo[s].keys(), writes=psk(bo))
                        P.op("dve", lambda e, bo=bo, tt=tt, half=half: e.scalar_tensor_tensor(out=h[:, tt, half * 512:(half + 1) * 512], in0=psb(bo), scalar=0.5, in1=h[:, tt, half * 512:(half + 1) * 512], op0=ALU.mult, op1=ALU.add),
                             reads=psk(bo) + [("h", tt)], writes=[("h", tt)])

            for g in range(NG):
                for blk in range(NB):
                    a = ai[0] % 3
                    ai[0] += 1
                    gateup(g, blk, a)
                    if pending[0] is not None:
                        outproj(*pending[0])
                        if pending[0][0] == NG - 1:
                            hook(pending[0][1])
                    pending[0] = (g, blk, a)
                    if blk == 0 and g + 2 < NG:
                        load_group(g + 2)
            outproj(*pending[0])
            hook(pending[0][1])

        def emit_xattn(layer, hook):
            wq_d = xa_wq_d[layer]
            wkv_d = xa_wkv_d[layer]
            wo_d = xa_wo_d[layer]
            CV.reset()
            wq = CV.take(KC * D, BF16)
            wo = CV.take(KC * D, BF16)
            wkv = [CV.take(KC * 512, BF16) for _ in range(2)]
            kT = CV.take(KC * MEM, BF16)
            vsb = CV.take(2 * D, BF16)
            qT = CV.take(KC * 512, BF16)
            oT = CV.take(KC * 512, BF16)
            expT = [CV.take(2 * 512, BF16) for _ in range(2)]
            rB = [CV.take(512, F32) for _ in range(2)]
            wqv = wq.ap.rearrange("p (f k n) -> p f k n", f=2, n=512)
            wov = wo.ap.rearrange("p (f k n) -> p f k n", f=2, n=512)
            kTv = kT.ap.rearrange("p (c m) -> p c m", m=MEM)
            vv = vsb.ap.rearrange("p (t n) -> p t n", n=D)
            qTv = qT.ap.rearrange("p (c n) -> p c n", n=512)
            oTv = oT.ap.rearrange("p (c n) -> p c n", n=512)

            def load_kvblk(i):
                wload(wkv[i % 2].ap.rearrange("p (k n) -> p k n", n=512), wkv_d[:, i * 512:(i + 1) * 512].rearrange("(k p) n -> p k n", p=128), wkv[i % 2].keys())
            load_kvblk(0)
            load_kvblk(1)
            for hf in range(2):
                wload(wqv[:, hf], wq_d[:, hf * 512:(hf + 1) * 512].rearrange("(k p) n -> p k n", p=128), wq.keys(hf * KC * 512, (hf + 1) * KC * 512))
            for i in range(4):
                wv = wkv[i % 2].ap.rearrange("p (k n) -> p k n", n=512)
                if i < 2:
                    for c4 in range(4):
                        b = P.bank()
                        for k in range(KC):
                            P.op("pe", lambda e, b=b, k=k, c4=c4, wv=wv: e.matmul(ps[:, b, 0:MEM], lhsT=wv[:, k, c4 * 128:(c4 + 1) * 128], rhs=memnT[:, k, :], start=(k == 0), stop=(k == KC - 1)),
                                 reads=wkv[i % 2].keys() + ["memnT"], writes=psk(b))
                        P.op("act", lambda e, b=b, c=i * 4 + c4: e.activation(out=kTv[:, c, :], in_=ps[:, b, 0:MEM], func=AF.Copy, scale=float(256 ** -0.5)), reads=psk(b), writes=kT.keys())
                else:
                    for mt in range(2):
                        b = P.bank()
                        for k in range(KC):
                            P.op("pe", lambda e, b=b, k=k, mt=mt, wv=wv: e.matmul(psb(b), lhsT=memnT[:, k, mt * 128:(mt + 1) * 128], rhs=wv[:, k, :], start=(k == 0), stop=(k == KC - 1)),
                                 reads=wkv[i % 2].keys() + ["memnT"], writes=psk(b))
                        P.op("dve", lambda e, b=b, mt=mt, i=i: e.tensor_copy(out=vv[:, mt, (i - 2) * 512:(i - 1) * 512], in_=psb(b)), reads=psk(b), writes=vsb.keys())
                if i + 2 < 4:
                    load_kvblk(i + 2)
            for hf in range(2):
                wload(wov[:, hf], wo_d[:, hf * 512:(hf + 1) * 512].rearrange("(k p) n -> p k n", p=128), wo.keys(hf * KC * 512, (hf + 1) * KC * 512))
            xi = [0]

            def qproj(blk):
                for c in range(KC):
                    b = P.bank()
                    for k in range(KC):
                        P.op("pe", lambda e, b=b, k=k, c=c: e.matmul(psb(b), lhsT=wqv[:, c // 4, k, (c % 4) * 128:(c % 4 + 1) * 128], rhs=hnT[:, k, blk * 512:(blk + 1) * 512], start=(k == 0), stop=(k == KC - 1)),
                             reads=wq.keys() + hnT_keys(blk), writes=psk(b))
                    if c % 2 == 0:
                        P.op("act", lambda e, b=b, c=c: e.activation(out=qTv[:, c, :], in_=psb(b), func=AF.Copy), reads=psk(b), writes=qT.keys(c * 512, (c + 1) * 512))
                    else:
                        P.op("dve", lambda e, b=b, c=c: e.tensor_copy(out=qTv[:, c, :], in_=psb(b)), reads=psk(b), writes=qT.keys(c * 512, (c + 1) * 512))

            def woproj(blk):
                for t4 in range(4):
                    tt = blk * 4 + t4
                    for half in range(2):
                        b = P.bank()
                        for c in range(KC):
                            P.op("pe", lambda e, b=b, c=c, t4=t4, half=half: e.matmul(psb(b), lhsT=oTv[:, c, t4 * 128:(t4 + 1) * 128], rhs=wov[:, half, c, :], start=(c == 0), stop=(c == KC - 1)),
                                 reads=oT.keys() + wo.keys(), writes=psk(b))
                        add_to_h(tt, half, b)

            pend_hook = [None]
            qproj(0)
            for blk in range(NB):
                def scores(hh, xs):
                    ev = expT[xs].ap.rearrange("p (t n) -> p t n", n=512)
                    for mt in range(2):
                        b = P.bank()
                        for j in range(2):
                            c = 2 * hh + j
                            P.op("pe", lambda e, b=b, j=j, c=c, mt=mt: e.matmul(psb(b), lhsT=kTv[:, c, mt * 128:(mt + 1) * 128], rhs=qTv[:, c, :], start=(j == 0), stop=(j == 1)),
                                 reads=kT.keys() + qT.keys(c * 512, (c + 1) * 512), writes=psk(b))
                        P.op("act", lambda e, b=b, mt=mt, ev=ev: e.activation(out=ev[:, mt, :], in_=psb(b), func=AF.Exp), reads=psk(b), writes=expT[xs].keys(mt * 512, (mt + 1) * 512))

                def attend(hh, xs):
                    ev = expT[xs].ap.rearrange("p (t n) -> p t n", n=512)
                    bd = P.bank()
                    for mt in range(2):
                        P.op("pe", lambda e, bd=bd, mt=mt, ev=ev: e.matmul(psb(bd), lhsT=ones[:], rhs=ev[:, mt, :], start=(mt == 0), stop=(mt == 1)),
                             reads=expT[xs].keys() + ["ones"], writes=psk(bd))
                    P.op("dve", lambda e, bd=bd, xs=xs: e.reciprocal(out=rB[xs].ap, in_=psb(bd)), reads=psk(bd), writes=rB[xs].keys())
                    for j in range(2):
                        c = 2 * hh + j
                        b = P.bank()
                        for mt in range(2):
                            P.op("pe", lambda e, b=b, c=c, mt=mt, ev=ev: e.matmul(psb(b), lhsT=vv[:, mt, c * 128:(c + 1) * 128], rhs=ev[:, mt, :], start=(mt == 0), stop=(mt == 1)),
                                 reads=vsb.keys() + expT[xs].keys(), writes=psk(b))
                        P.op("dve", lambda e, b=b, c=c, xs=xs: e.tensor_tensor(out=oTv[:, c, :], in0=psb(b), in1=rB[xs].ap, op=ALU.mult), reads=psk(b) + rB[xs].keys(), writes=oT.keys(c * 512, (c + 1) * 512))

                scores(0, 0)
                for hh in range(4):
                    if hh + 1 < 4:
                        scores(hh + 1, (hh + 1) % 2)
                    attend(hh, hh % 2)
                if blk + 1 < NB:
                    qproj(blk + 1)
                if pend_hook[0] is not None:
                    hook(pend_hook[0])
                    pend_hook[0] = None
                woproj(blk)
                if hasattr(hook, "pre"):
                    hook.pre(blk)
                pend_hook[0] = blk
            hook(pend_hook[0])

        def emit_conv(hook):
            CV.reset()
            TP = T + PAD
            uT = CV.take(KC * TP, BF16)
            wout = CV.take(KC * D, BF16)
            mark = CV.off
            wi = [CV.take(KC * 256, BF16) for _ in range(3)]
            sg = [CV.take(512, BF16) for _ in range(2)]
            CV.off = mark
            dg = [CV.take(CW * 128, BF16) for _ in range(2)]
            u2b = [CV.take(512, BF16) for _ in range(2)]
            sq = [CV.take(512, BF16) for _ in range(2)]
            nrm = [CV.take(512, F32) for _ in range(2)]
            CH = Carver(arena, HNT_B) if HNT_B >= 31 * 1024 else CV
            u2 = CH.take(KC * 512, F32)
            yT = CH.take(KC * 512, BF16)
            meanB = CH.take(512, F32)
            rstdB = CH.take(512, F32)
            tmpB = CH.take(512, F32)
            uTv = uT.ap.rearrange("p (c t) -> p c t", t=TP)
            woutv = wout.ap.rearrange("p (f k n) -> p f k n", f=2, n=512)
            u2v = u2.ap.rearrange("p (c n) -> p c n", n=512)
            yTv = yT.ap.rearrange("p (c n) -> p c n", n=512)
            P.op("pool", lambda e: e.memset(uTv[:, :, 0:PAD], 0.0), writes=uT.keys())
            g = gslot[0] % 2
            gslot[0] += 1
            load_gain(g, cv_bout_d[0])
            def bout_add(tt):
                P.op("pool", lambda e: e.tensor_tensor(out=h[:, tt, :], in0=h[:, tt, :], in1=gB[:, g, :], op=ALU.add), reads=[("h", tt), ("gB", g)], writes=[("h", tt)])

            def load_wi(c):
                s = c % 3
                v = wi[s].ap.rearrange("p (a k n) -> p a k n", a=2, n=128)
                wload(v[:, 0], cv_win_d[0][:, c * 128:(c + 1) * 128].rearrange("(k p) n -> p k n", p=128), wi[s].keys(0, KC * 128))
                wload(v[:, 1], cv_win_d[0][:, D + c * 128:D + (c + 1) * 128].rearrange("(k p) n -> p k n", p=128), wi[s].keys(KC * 128, 2 * KC * 128))
            load_wi(0)
            load_wi(1)
            si = [0]
            for c in range(KC):
                if c + 2 < KC:
                    load_wi(c + 2)
                for tt in range(c * NT // KC, (c + 1) * NT // KC):
                    bout_add(tt)
                v = wi[c % 3].ap.rearrange("p (a k n) -> p a k n", a=2, n=128)
                for blk in range(NB):
                    ba = P.bank()
                    bg = P.bank()
                    for (bb, a) in ((ba, 0), (bg, 1)):
                        for k in range(KC):
                            P.op("pe", lambda e, bb=bb, a=a, k=k, v=v, blk=blk: e.matmul(psb(bb), lhsT=v[:, a, k, :], rhs=hnT[:, k, blk * 512:(blk + 1) * 512], start=(k == 0), stop=(k == KC - 1)),
                                 reads=wi[c % 3].keys() + hnT_keys(blk), writes=psk(bb))
                    s = si[0] % 2
                    si[0] += 1
                    P.op("act", lambda e, bg=bg, s=s, c=c: e.activation(out=sg[s].ap, in_=psb(bg), func=AF.Sigmoid, bias=vecs[:, V_BG, c:c + 1]), reads=psk(bg) + [("vec", V_BG)], writes=sg[s].keys())
                    lo = c * TP + PAD + blk * 512
                    P.op("dve", lambda e, ba=ba, s=s, c=c, blk=blk: e.scalar_tensor_tensor(out=uTv[:, c, PAD + blk * 512:PAD + (blk + 1) * 512], in0=psb(ba), scalar=vecs[:, V_BA, c:c + 1], in1=sg[s].ap, op0=ALU.add, op1=ALU.mult),
                         reads=psk(ba) + sg[s].keys() + [("vec", V_BA)], writes=uT.keys(lo, lo + 512))
            for hf in range(2):
                wload(woutv[:, hf], cv_wout_d[0][:, hf * 512:(hf + 1) * 512].rearrange("(k p) n -> p k n", p=128), wout.keys(hf * KC * 512, (hf + 1) * KC * 512))
            di = [0]

            def conv_outproj(blk):
                for t4 in range(4):
                    tt = blk * 4 + t4
                    for half in range(2):
                        b = P.bank()
                        for c in range(KC):
                            P.op("pe", lambda e, b=b, c=c, t4=t4, half=half: e.matmul(psb(b), lhsT=yTv[:, c, t4 * 128:(t4 + 1) * 128], rhs=woutv[:, half, c, :], start=(c == 0), stop=(c == KC - 1)),
                                 reads=yT.keys() + wout.keys(), writes=psk(b))
                        add_to_h(tt, half, b)
            pend_out = [None]

            def build_dg(c):
                d = di[0] % 2
                di[0] += 1
                dgv = dg[d].ap.rearrange("p (j n) -> p j n", n=128)
                NP_ = 21
                P.op("pool", lambda e: e.tensor_tensor(out=dgv[:, 0:NP_, :], in0=identf[:, None, :].to_broadcast([128, NP_, 128]), in1=dwT[:, c, 0:NP_, None].to_broadcast([128, NP_, 128]), op=ALU.mult),
                     reads=["identf", "dwT"], writes=dg[d].keys(0, NP_ * 128))
                P.op("dve", lambda e: e.tensor_tensor(out=dgv[:, NP_:CW, :], in0=identf[:, None, :].to_broadcast([128, CW - NP_, 128]), in1=dwT[:, c, NP_:CW, None].to_broadcast([128, CW - NP_, 128]), op=ALU.mult),
                     reads=["identf", "dwT"], writes=dg[d].keys(NP_ * 128, CW * 128))
                return d, dgv
            next_dg = [build_dg(0)]

            def normalize_chunk(c):
                n = c % 2
                P.op("dve", lambda e: e.tensor_tensor(out=nrm[n].ap, in0=u2v[:, c, :], in1=meanB.ap, op=ALU.subtract), reads=u2.keys(c * 512, (c + 1) * 512) + meanB.keys(), writes=nrm[n].keys())
                P.op("dve", lambda e: e.tensor_tensor(out=nrm[n].ap, in0=nrm[n].ap, in1=rstdB.ap, op=ALU.mult), reads=nrm[n].keys() + rstdB.keys(), writes=nrm[n].keys())
                P.op("dve", lambda e: e.tensor_scalar(out=nrm[n].ap, in0=nrm[n].ap, scalar1=vecs[:, V_LNG, c:c + 1], scalar2=vecs[:, V_LNB, c:c + 1], op0=ALU.mult, op1=ALU.add),
                     reads=nrm[n].keys() + [("vec", V_LNG), ("vec", V_LNB)], writes=nrm[n].keys())
                P.op("act", lambda e: e.activation(out=yTv[:, c, :], in_=nrm[n].ap, func=AF.Silu), reads=nrm[n].keys(), writes=yT.keys(c * 512, (c + 1) * 512))
            for blk in range(NB):
                bs = P.bank(pin=True)
                bq = P.bank(pin=True)
                pend_stats = [None]
                for c in range(KC):
                    d, dgv = next_dg[0]
                    if not (blk == NB - 1 and c == KC - 1):
                        next_dg[0] = build_dg((c + 1) % KC)
                    b = P.bank()
                    off = PAD - (CW - 1)
                    for j in range(CW):
                        lo = blk * 512 + j + off
                        P.op("pe", lambda e, b=b, j=j, c=c, lo=lo, dgv=dgv: e.matmul(psb(b), lhsT=dgv[:, j, :], rhs=uTv[:, c, lo:lo + 512], start=(j == 0), stop=(j == CW - 1)),
                             reads=dg[d].keys() + uT.keys(c * TP + lo, c * TP + lo + 512), writes=psk(b))
                    if pend_out[0] is not None:
                        normalize_chunk(c)
                    P.op("dve", lambda e, b=b, c=c: e.tensor_scalar(out=u2v[:, c, :], in0=psb(b), scalar1=vecs[:, V_DWB, c:c + 1], scalar2=None, op0=ALU.add), reads=psk(b) + [("vec", V_DWB)], writes=u2.keys(c * 512, (c + 1) * 512))
                    P.op("act", lambda e, c=c, d=d: e.activation(out=u2b[d].ap, in_=u2v[:, c, :], func=AF.Copy), reads=u2.keys(c * 512, (c + 1) * 512), writes=u2b[d].keys())
                    P.op("act", lambda e, c=c, d=d: e.activation(out=sq[d].ap, in_=u2v[:, c, :], func=AF.Square), reads=u2.keys(c * 512, (c + 1) * 512), writes=sq[d].keys())
                    if pend_out[0] is not None and c == KC - 1:
                        conv_outproj(pend_out[0])
                        pend_out[0] = None
                    def stats_mm(bs=bs, bq=bq, c=c, d=d):
                        P.op("pe", lambda e: e.matmul(psb(bs), lhsT=ones[:], rhs=u2b[d].ap, start=(c == 0), stop=(c == KC - 1)), reads=u2b[d].keys() + ["ones"], writes=psk(bs))
                        P.op("pe", lambda e: e.matmul(psb(bq), lhsT=ones[:], rhs=sq[d].ap, start=(c == 0), stop=(c == KC - 1)), reads=sq[d].keys() + ["ones"], writes=psk(bq))
                    if pend_stats[0] is not None:
                        pend_stats[0]()
                    pend_stats[0] = stats_mm
                pend_stats[0]()
                P.op("act", lambda e, bs=bs: e.activation(out=meanB.ap, in_=psb(bs), func=AF.Copy, scale=1.0 / D), reads=psk(bs), writes=meanB.keys())
                P.op("dve", lambda e: e.tensor_tensor(out=tmpB.ap, in0=meanB.ap, in1=meanB.ap, op=ALU.mult), reads=meanB.keys(), writes=tmpB.keys())
                P.op("dve", lambda e, bq=bq: e.scalar_tensor_tensor(out=tmpB.ap, in0=psb(bq), scalar=1.0 / D, in1=tmpB.ap, op0=ALU.mult, op1=ALU.subtract), reads=psk(bq) + tmpB.keys(), writes=tmpB.keys())
                P.unpin(bs)
                P.unpin(bq)
                P.op("act", lambda e: e.activation(out=tmpB.ap, in_=tmpB.ap, func=AF.Sqrt, bias=LN_EPS), reads=tmpB.keys(), writes=tmpB.keys())
                P.op("dve", lambda e: e.reciprocal(out=rstdB.ap, in_=tmpB.ap), reads=tmpB.keys(), writes=rstdB.keys())
                pend_out[0] = blk
            for c in range(KC):
                normalize_chunk(c)
            conv_outproj(pend_out[0])
            for blk in range(NB):
                hook(blk)

        def emit_hgrn(hook):
            CV.reset()
            wi = [CV.take(KC * 512, BF16) for _ in range(2)]
            wo = [CV.take(D, BF16) for _ in range(3)]
            NSL = 2
            SQ = [CV.take(512, F32) for _ in range(NSL)]
            SG = [CV.take(512, F32) for _ in range(NSL)]
            A = [CV.take(512, F32) for _ in range(NSL)]
            Bf = [CV.take(512, F32) for _ in range(NSL)]
            Cf = [CV.take(512, F32) for _ in range(NSL)]
            qs = [CV.take(512, BF16) for _ in range(NSL)]
            gs = [CV.take(512, BF16) for _ in range(NSL)]
            qg = [CV.take(512, BF16) for _ in range(NSL)]
            kg = [CV.take(512, BF16) for _ in range(NSL)]
            vt = [CV.take(512, BF16) for _ in range(NSL)]
            scT = [CV.take(512, BF16) for _ in range(NSL)]
            kgt = [CV.take(512, BF16) for _ in range(NSL)]
            dSe = [CV.take(512, F32) for _ in range(NSL)]
            stF = [CV.take(512, F32) for _ in range(NSL)]
            stB = [CV.take(512, BF16) for _ in range(NSL)]
            sqo = [CV.take(512, BF16) for _ in range(2)]
            rsB = [CV.take(512, F32) for _ in range(2)]
            og = [CV.take(512, F32) for _ in range(2)]
            ogT = [CV.take(512, BF16) for _ in range(2)]

            def load_head(hd):
                s = hd % 2
                v = wi[s].ap.rearrange("p (a k n) -> p a k n", a=4, n=128)
                for a in range(4):
                    wload(v[:, a], hg_win_d[0][:, a * D + hd * 128:a * D + (hd + 1) * 128].rearrange("(k p) n -> p k n", p=128), wi[s].keys(a * KC * 128, (a + 1) * KC * 128))
                wload(wo[hd % 3].ap, hg_wout_d[0][hd * 128:(hd + 1) * 128, :], wo[hd % 3].keys())

            def proj(hd, blk, u):
                s = hd % 2
                v = wi[s].ap.rearrange("p (a k n) -> p a k n", a=4, n=128)
                lbc = vecs[:, V_LB, hd:hd + 1]
                omlc = vecs[:, V_OML, hd:hd + 1]
                nomlc = vecs[:, V_NOML, hd:hd + 1]

                def grp(a):
                    bb = P.bank()
                    for k in range(KC):
                        P.op("pe", lambda e, bb=bb, a=a, k=k: e.matmul(psb(bb), lhsT=v[:, a, k, :], rhs=hnT[:, k, blk * 512:(blk + 1) * 512], start=(k == 0), stop=(k == KC - 1)),
                             reads=wi[s].keys(a * KC * 128, (a + 1) * KC * 128) + hnT_keys(blk), writes=psk(bb))
                    return bb
                bf = grp(1)
                bq = grp(0)
                bgt = grp(3)
                P.op("act", lambda e: e.activation(out=A[u].ap, in_=psb(bf), func=AF.Sigmoid), reads=psk(bf), writes=A[u].keys())
                P.op("act", lambda e: e.activation(out=SQ[u].ap, in_=psb(bq), func=AF.Sigmoid), reads=psk(bq), writes=SQ[u].keys())
                P.op("act", lambda e: e.activation(out=SG[u].ap, in_=psb(bgt), func=AF.Sigmoid), reads=psk(bgt), writes=SG[u].keys())
                P.op("act", lambda e: e.activation(out=Bf[u].ap, in_=A[u].ap, func=AF.Ln, scale=omlc, bias=lbc), reads=A[u].keys() + [("vec", V_LB), ("vec", V_OML)], writes=Bf[u].keys())
                P.op("dve", lambda e: e.tensor_tensor(out=SQ[u].ap, in0=psb(bq), in1=SQ[u].ap, op=ALU.mult), reads=psk(bq) + SQ[u].keys(), writes=SQ[u].keys())
                P.op("dve", lambda e: e.tensor_tensor(out=gs[u].ap, in0=psb(bgt), in1=SG[u].ap, op=ALU.mult), reads=psk(bgt) + SG[u].keys(), writes=gs[u].keys())
                P.op("pool", lambda e: e.tensor_scalar(out=A[u].ap, in0=A[u].ap, scalar1=nomlc, scalar2=omlc, op0=ALU.mult, op1=ALU.add), reads=A[u].keys() + [("vec", V_NOML), ("vec", V_OML)], writes=A[u].keys())
                P.op("dve", lambda e: e.tensor_tensor_scan(out=Cf[u].ap, data0=rmask[:], data1=Bf[u].ap, initial=0.0, op0=ALU.mult, op1=ALU.add), reads=Bf[u].keys() + ["rmask"], writes=Cf[u].keys())
                P.op("act", lambda e: e.activation(out=Bf[u].ap, in_=Cf[u].ap, func=AF.Exp), reads=Cf[u].keys(), writes=Bf[u].keys())
                P.op("act", lambda e: e.activation(out=Cf[u].ap, in_=Cf[u].ap, func=AF.Exp, scale=-1.0), reads=Cf[u].keys(), writes=Cf[u].keys())
                P.op("pool", lambda e: e.tensor_tensor(out=qg[u].ap, in0=SQ[u].ap, in1=Bf[u].ap, op=ALU.mult), reads=SQ[u].keys() + Bf[u].keys(), writes=qg[u].keys())
                P.op("pool", lambda e: e.tensor_tensor(out=kg[u].ap, in0=A[u].ap, in1=Cf[u].ap, op=ALU.mult), reads=A[u].keys() + Cf[u].keys(), writes=kg[u].keys())
                yield
                bv = P.bank()
                for t4 in range(4):
                    for k in range(KC):
                        P.op("pe", lambda e, t4=t4, k=k: e.matmul(ps[:, bv, t4 * 128:(t4 + 1) * 128], lhsT=hnT[:, k, blk * 512 + t4 * 128:blk * 512 + (t4 + 1) * 128], rhs=v[:, 2, k, :], start=(k == 0), stop=(k == KC - 1)),
                             reads=wi[s].keys(2 * KC * 128, 3 * KC * 128) + hnT_keys(blk), writes=psk(bv))
                P.op("dve", lambda e: e.tensor_copy(out=vt[u].ap, in_=psb(bv)), reads=psk(bv), writes=vt[u].keys())
                if dump is not None and hd == dump[0] and blk == dump[1]:
                    for di_, bufd in enumerate((A[u], Bf[u], Cf[u], qg[u], kg[u], vt[u], gs[u])):
                        P.op("pool", lambda e, di_=di_, bufd=bufd: e.dma_start(out=dbg_d[:, di_, :], in_=bufd.ap), reads=bufd.keys(), writes=[("dbg", di_)], dma=True)
                yield

            ri = [0]

            def rec(hd, blk, u):
                s = hd % 2
                vtv = vt[u].ap.rearrange("p (t n) -> p t n", n=128)
                scTv = scT[u].ap.rearrange("p (t n) -> p t n", n=128)
                dSev = dSe[u].ap.rearrange("p (t n) -> p t n", n=128)
                stFv = stF[u].ap.rearrange("p (t n) -> p t n", n=128)
                stBv = stB[u].ap.rearrange("p (t n) -> p t n", n=128)
                if blk == 0:
                    carry_f, carry_b, carry_k = stf[:], stb[:], ["stf", "stb"]
                else:
                    pf = stF[1 - u].ap.rearrange("p (t n) -> p t n", n=128)
                    pb = stB[1 - u].ap.rearrange("p (t n) -> p t n", n=128)
                    carry_f, carry_b, carry_k = pf[:, 3, :], pb[:, 3, :], stF[1 - u].keys() + stB[1 - u].keys()
                bsc = P.bank()
                for c in range(4):
                    cs = slice(c * 128, (c + 1) * 128)
                    P.op("pe", lambda e, cs=cs: e.matmul(ps[:, bsc, cs], lhsT=kg[u].ap[:, cs], rhs=qg[u].ap[:, cs], start=True, stop=True), reads=kg[u].keys() + qg[u].keys(), writes=psk(bsc))
                bkt = P.bank()
                for c in range(4):
                    cs = slice(c * 128, (c + 1) * 128)
                    P.op("pe", lambda e, cs=cs: e.transpose(out=psb16(bkt)[:, cs], in_=kg[u].ap[:, cs], identity=ident[:]), reads=kg[u].keys() + ["ident"], writes=psk(bkt))
                P.op("dve", lambda e: e.tensor_tensor(out=scTv, in0=ps[:, bsc, :].rearrange("p (t n) -> p t n", n=128), in1=maskT[:, None, :].to_broadcast([128, 4, 128]), op=ALU.mult), reads=psk(bsc) + ["maskT"], writes=scT[u].keys())
                P.op("act", lambda e: e.activation(out=kgt[u].ap, in_=psb16(bkt)[:, 0:512], func=AF.Copy), reads=psk(bkt), writes=kgt[u].keys())
                yield
                bds = P.bank()
                for c in range(4):
                    cs = slice(c * 128, (c + 1) * 128)
                    P.op("pe", lambda e, c=c, cs=cs: e.matmul(ps[:, bds, cs], lhsT=kgt[u].ap[:, cs], rhs=vtv[:, c, :], start=True, stop=True), reads=kgt[u].keys() + vt[u].keys(), writes=psk(bds))
                egls = [Bf[u].ap[:, c * 128 + 127:c * 128 + 128] for c in range(4)]
                for c in range(4):
                    cs = slice(c * 128, (c + 1) * 128)
                    P.op("dve", lambda e, c=c, cs=cs: e.tensor_scalar(out=dSev[:, c, :], in0=ps[:, bds, cs], scalar1=egls[c], scalar2=None, op0=ALU.mult), reads=psk(bds) + Bf[u].keys(), writes=dSe[u].keys(c * 128, (c + 1) * 128))
                for c in range(4):
                    prev = carry_f if c == 0 else stFv[:, c - 1, :]
                    pk = carry_k if c == 0 else stF[u].keys((c - 1) * 128, c * 128)
                    P.op("dve", lambda e, c=c, prev=prev: e.scalar_tensor_tensor(out=stFv[:, c, :], in0=prev, scalar=egls[c], in1=dSev[:, c, :], op0=ALU.mult, op1=ALU.add),
                         reads=pk + Bf[u].keys() + dSe[u].keys(c * 128, (c + 1) * 128), writes=stF[u].keys(c * 128, (c + 1) * 128))
                    P.op("dve", lambda e, c=c, prev=prev: e.scalar_tensor_tensor(out=stBv[:, c, :], in0=prev, scalar=egls[c], in1=dSev[:, c, :], op0=ALU.mult, op1=ALU.add),
                         reads=pk + Bf[u].keys() + dSe[u].keys(c * 128, (c + 1) * 128), writes=stB[u].keys(c * 128, (c + 1) * 128))
                yield
                bo = P.bank()
                for c in range(4):
                    cs = slice(c * 128, (c + 1) * 128)
                    P.op("pe", lambda e, c=c, cs=cs: e.matmul(ps[:, bo, cs], lhsT=vtv[:, c, :], rhs=scTv[:, c, :], start=True, stop=False), reads=vt[u].keys() + scT[u].keys(), writes=psk(bo))
                    sb_prev = carry_b if c == 0 else stBv[:, c - 1, :]
                    sk = carry_k if c == 0 else stB[u].keys((c - 1) * 128, c * 128)
                    P.op("pe", lambda e, cs=cs, sb_prev=sb_prev: e.matmul(ps[:, bo, cs], lhsT=sb_prev, rhs=qg[u].ap[:, cs], start=False, stop=True), reads=sk + qg[u].keys(), writes=psk(bo))
                o = ri[0] % 2
                ri[0] += 1
                P.op("act", lambda e: e.activation(out=sqo[o].ap, in_=psb(bo), func=AF.Square), reads=psk(bo), writes=sqo[o].keys())
                bs = P.bank()
                P.op("pe", lambda e: e.matmul(psb(bs), lhsT=ones[:], rhs=sqo[o].ap, start=True, stop=True), reads=sqo[o].keys() + ["ones"], writes=psk(bs))
                P.op("act", lambda e: e.activation(out=rsB[o].ap, in_=psb(bs), func=AF.Ln, bias=RMS_EPS, scale=1.0 / 128), reads=psk(bs), writes=rsB[o].keys())
                P.op("act", lambda e: e.activation(out=rsB[o].ap, in_=rsB[o].ap, func=AF.Exp, scale=-0.5), reads=rsB[o].keys(), writes=rsB[o].keys())
                P.op("dve", lambda e: e.tensor_tensor(out=og[o].ap, in0=psb(bo), in1=rsB[o].ap, op=ALU.mult), reads=psk(bo) + rsB[o].keys(), writes=og[o].keys())
                P.op("dve", lambda e: e.scalar_tensor_tensor(out=ogT[o].ap, in0=og[o].ap, scalar=vecs[:, V_HN, hd:hd + 1], in1=gs[u].ap, op0=ALU.mult, op1=ALU.mult), reads=og[o].keys() + gs[u].keys() + [("vec", V_HN)], writes=ogT[o].keys())
                rec_o[(hd, blk)] = o
                yield

            rec_o = {}

            def outp(hd, blk):
                o = rec_o[(hd, blk)]
                w = wo[hd % 3]
                for t4 in range(4):
                    tt = blk * 4 + t4
                    for half in range(2):
                        b = P.bank()
                        P.op("pe", lambda e, b=b, t4=t4, half=half: e.matmul(psb(b), lhsT=ogT[o].ap[:, t4 * 128:(t4 + 1) * 128], rhs=w.ap[:, half * 512:(half + 1) * 512], start=True, stop=True),
                             reads=ogT[o].keys() + w.keys(), writes=psk(b))
                        add_to_h(tt, half, b)
                    yield
                if hd == KC - 1:
                    hook(blk)

            def interleave(gens):
                alive = [g for g in gens if g is not None]
                while alive:
                    for gen in list(alive):
                        try:
                            next(gen)
                        except StopIteration:
                            alive.remove(gen)

            units = [(hd, blk) for hd in range(KC) for blk in range(NB)]
            load_head(0)
            def step(gen):
                if gen is not None:
                    try:
                        next(gen)
                    except StopIteration:
                        pass

            def drain(gen):
                if gen is not None:
                    for _ in gen:
                        pass
            n = len(units)
            for i in range(n + 2):
                gp = proj(units[i][0], units[i][1], i % NSL) if i < n else None
                gr = rec(units[i - 1][0], units[i - 1][1], (i - 1) % NSL) if 1 <= i <= n else None
                go = outp(*units[i - 2]) if 2 <= i <= n + 1 else None
                step(gr)
                step(gr)
                step(gp)
                step(go)
                step(go)
                step(gp)
                drain(go)
                drain(gr)
                drain(gp)
                if i < n and units[i][1] == 0 and units[i][0] + 1 < KC:
                    load_head(units[i][0] + 1)

        for sq_i in range(NSEQ):
            if any(st.startswith("xattn") for st in stages):
                g = gslot[0] % 2
                gslot[0] += 1
                load_gain(g, mem_norm_d)
                P.op("sp", lambda e, sq_i=sq_i: e.dma_start(out=memt[:], in_=mem_d[sq_i].rearrange("(t p) d -> p t d", p=128)), writes=memt_buf.keys(), dma=True)
                for mt in range(2):
                    norm_stats(memt[:, mt, :], 16 + mt, memt_buf.keys())
                P.op("act", lambda e: e.activation(out=stat[:, 1, 16:18], in_=stat[:, 0, 16:18], func=AF.Sqrt, bias=RMS_EPS, scale=1.0 / D), reads=[("ss", 16), ("ss", 17)], writes=[("sd", "m")])
                P.op("dve", lambda e: e.reciprocal(out=stat[:, 2, 16:18], in_=stat[:, 1, 16:18]), reads=[("sd", "m")], writes=[("rstd", "m")])
                for mt in range(2):
                    s = mt % 2
                    P.op("dve", lambda e, mt=mt, s=s, g=g: e.scalar_tensor_tensor(out=hn_tok[:, s, :], in0=memt[:, mt, :], scalar=stat[:, 2, 16 + mt:17 + mt], in1=gB[:, g, :], op0=ALU.mult, op1=ALU.mult),
                         reads=memt_buf.keys() + [("rstd", "m"), ("gB", g)], writes=[("hn_tok", s)])
                    b = P.bank()
                    for c in range(KC):
                        P.op("pe", lambda e, c=c, s=s, b=b: e.transpose(out=psb16(b)[:, c * 128:(c + 1) * 128], in_=hn_tok[:, s, c * 128:(c + 1) * 128], identity=ident[:]),
                             reads=[("hn_tok", s), "ident"], writes=psk(b))
                    P.op("dve", lambda e, mt=mt, b=b: e.tensor_copy(out=memnT[:, :, mt * 128:(mt + 1) * 128], in_=psb16(b).rearrange("p (c t) -> p c t", t=128)), reads=psk(b), writes=["memnT"])
            seq_stages = []
            for layer in range(2):
                if "ffn1_%d" % layer in stages:
                    seq_stages.append((ffn_norm_d[0][layer], lambda hook, layer=layer: emit_ffn(0, layer, hook)))
                if "mix_%d" % layer in stages:
                    seq_stages.append((mix_norm_d[layer], (lambda hook: emit_hgrn(hook)) if layer == 0 else (lambda hook: emit_conv(hook))))
                if "xattn_%d" % layer in stages:
                    seq_stages.append((xa_norm_d[layer], lambda hook, layer=layer: emit_xattn(layer, hook)))
                if "ffn2_%d" % layer in stages:
                    seq_stages.append((ffn_norm_d[1][layer], lambda hook, layer=layer: emit_ffn(1, layer, hook)))
            has_final = "final" in stages

            def store_block(blk, sq_i=sq_i):
                for tt in range(blk * 4, blk * 4 + 4):
                    P.op("sp", lambda e, tt=tt: e.dma_start(out=y_d[sq_i, tt * 128:(tt + 1) * 128, :], in_=h[:, tt, :]), reads=[("h", tt)], writes=[("y", sq_i, tt)], dma=True)

            def load_x_block(sq, blk):
                P.op("sp", lambda e: e.dma_start(out=h[:, blk * 4:(blk + 1) * 4, :], in_=x_d[sq, blk * 512:(blk + 1) * 512, :].rearrange("(t p) d -> p t d", p=128)),
                     writes=[("h", blk * 4 + i) for i in range(4)], dma=True)

            def tail_hook_factory(sq_i=sq_i):
                if has_final:
                    gf = begin_norm(fin_norm_d)
                    inner = lambda blk: final_block(blk, gf, sq_i)
                else:
                    inner = store_block

                def tail(blk):
                    inner(blk)
                    if sq_i + 1 < NSEQ:
                        load_x_block(sq_i + 1, blk)
                return tail

            if not seq_stages:
                hk = tail_hook_factory()
                for blk in range(NB):
                    hk(blk)
            else:
                g0 = begin_norm(seq_stages[0][0])
                for blk in range(NB):
                    norm_block(blk, g0)
                for i, (gsrc, emit_fn) in enumerate(seq_stages):
                    if i + 1 < len(seq_stages):
                        gn = begin_norm(seq_stages[i + 1][0])
                        pre_done = set()

                        def hk(blk, gn=gn, pre_done=pre_done):
                            norm_block(blk, gn, skip_rstd=(blk in pre_done))

                        def hk_pre(blk, pre_done=pre_done):
                            block_rstd(blk)
                            pre_done.add(blk)
                        hk.pre = hk_pre
                    else:
                        hk = tail_hook_factory()
                    emit_fn(hk)

        run = P.emit(sems, dma_sems)

        @block.sync
        def _(e):
            run("sp", e)

        @block.scalar
        def _(e):
            run("act", e)

        @block.vector
        def _(e):
            run("dve", e)

        @block.gpsimd
        def _(e):
            run("pool", e)

        @block.tensor
        def _(e):
            run("pe", e)
        build.stats = P.stats
    return nc


_IN_NAMES = ["x", "mem", "ffn1_norm", "ffn1_w_in", "ffn1_w_out", "mix_norm", "hgrn_w_in", "hgrn_head_norm",
             "hgrn_w_out", "hgrn_lb_logits", "conv_w_in", "conv_b_in", "conv_dw", "conv_dw_b", "conv_ln_g",
             "conv_ln_b", "conv_w_out", "conv_b_out", "xattn_norm", "xattn_wq", "xattn_wkv", "xattn_wo",
             "ffn2_norm", "ffn2_w_in", "ffn2_w_out", "mem_norm", "final_norm"]


def kernel(**inputs):
    arrs = {k: np.ascontiguousarray(np.asarray(inputs[k], dtype=np.float32)) for k in _IN_NAMES}
    B, T, _ = arrs["x"].shape
    nseq = B // N_CORES
    nc = build(T=T, NSEQ=nseq)
    in_maps = []
    for c in range(N_CORES):
        m = dict(arrs)
        m["x"] = np.ascontiguousarray(arrs["x"][c * nseq:(c + 1) * nseq])
        m["mem"] = np.ascontiguousarray(arrs["mem"][c * nseq:(c + 1) * nseq])
        in_maps.append(m)
    res = run_bass_kernel_spmd(nc, in_maps, core_ids=list(range(N_CORES)))
    return np.concatenate([r["y"] for r in res.results], axis=0).astype(np.float32)
```

```python
import numpy as np
import concourse.bass as bass
import concourse.mybir as mybir
from concourse.bass_utils import run_bass_kernel_spmd

F32 = mybir.dt.float32
BF16 = mybir.dt.bfloat16
AF = mybir.ActivationFunctionType
ALU = mybir.AluOpType

D = 1024
KC = 8
FF = 2816
FC = 22
MEM = 256
CW = 31
PAD = 32
N_CORES = 8
RMS_EPS = 1e-6
LN_EPS = 1e-5
PG = 256

ENGS = ("pe", "act", "dve", "pool", "sp")


class Prog:
    def __init__(self):
        self.ops = []
        self.last_w = {}
        self.readers = {}
        self.bank_rr = 0
        self.pinned = set()

    def bank(self, pin=False):
        while True:
            b = self.bank_rr % 8
            self.bank_rr += 1
            if b not in self.pinned:
                break
        if pin:
            self.pinned.add(b)
        return b

    def unpin(self, b):
        self.pinned.discard(b)

    def op(self, eng, fn, reads=(), writes=(), dma=False):
        idx = len(self.ops)
        deps = {}

        def add(i):
            p = self.ops[i]
            if p["dma"]:
                deps[("d", i)] = i
            else:
                e = p["eng"]
                if deps.get(e, -1) < i:
                    deps[e] = i
        for k in reads:
            w = self.last_w.get(k)
            if w is not None:
                add(w)
        for k in writes:
            w = self.last_w.get(k)
            if w is not None:
                add(w)
            for r in self.readers.get(k, {}).values():
                add(r)
        o = dict(eng=eng, fn=fn, deps=list(deps.values()), dma=dma, signal=False)
        self.ops.append(o)
        for k in reads:
            self.readers.setdefault(k, {})[("d", idx) if dma else eng] = idx
        for k in writes:
            self.last_w[k] = idx
            self.readers[k] = {}
        return idx

    def emit(self, sems, dma_sems):
        ops = self.ops
        for o in ops:
            for d in o["deps"]:
                p = ops[d]
                if p["eng"] == "pe" and o["eng"] == "pe" and not p["dma"] and not o["dma"]:
                    continue
                p["signal"] = True
        cnt = {e: 0 for e in ENGS}
        slot_cnt = [0] * len(dma_sems)
        half = len(dma_sems) // 2
        slot_rr = {"sp": 0, "pool": 0}
        for o in ops:
            if o["dma"]:
                s = slot_rr[o["eng"]] % half + (0 if o["eng"] == "sp" else half)
                slot_rr[o["eng"]] += 1
                o["prev"] = slot_cnt[s]
                slot_cnt[s] += 16
                o["sem"] = ("dma", s)
                o["val"] = slot_cnt[s]
            elif o["signal"]:
                cnt[o["eng"]] += 1
                o["sem"] = ("eng", o["eng"])
                o["val"] = cnt[o["eng"]]
        per_eng = {e: [o for o in ops if o["eng"] == e] for e in ENGS}
        self.stats = {e: len(per_eng[e]) for e in ENGS}
        self.stats["signals"] = dict(cnt)

        def semobj(s):
            return dma_sems[s[1]] if s[0] == "dma" else sems[s[1]]

        def run(eng_name, eng):
            waited = {}
            for o in per_eng[eng_name]:
                need = {}
                for d in o["deps"]:
                    p = ops[d]
                    if p["eng"] == "pe" and eng_name == "pe" and not p["dma"] and not o["dma"]:
                        continue
                    s = p["sem"]
                    if need.get(s, 0) < p["val"]:
                        need[s] = p["val"]
                if o["dma"] and o["prev"] > 0:
                    s = o["sem"]
                    if need.get(s, 0) < o["prev"]:
                        need[s] = o["prev"]
                for s, v in need.items():
                    if waited.get(s, 0) >= v:
                        continue
                    eng.wait_ge(semobj(s), v)
                    waited[s] = v
                ins = o["fn"](eng)
                if o["dma"]:
                    ins.then_inc(semobj(o["sem"]), 16)
                elif o["signal"]:
                    ins.then_inc(semobj(o["sem"]), 1)
            for o in per_eng[eng_name]:
                if o["dma"]:
                    s = o["sem"]
                    if waited.get(s, 0) < o["val"]:
                        eng.wait_ge(semobj(s), o["val"])
                        waited[s] = o["val"]
        return run


class ABuf:
    def __init__(self, arena, off_bytes, n, dtype):
        esz = 4 if dtype == F32 else 2
        assert off_bytes % 4 == 0
        nb = n * esz
        assert nb % 4 == 0
        self.off = off_bytes
        self.esz = esz
        self.n = n
        a = arena[:, off_bytes // 4:(off_bytes + nb) // 4]
        self.ap = a if dtype == F32 else a.bitcast(BF16)
        assert tuple(self.ap.shape) == (128, n), (self.ap.shape, n)

    def keys(self, lo=0, hi=None):
        hi = self.n if hi is None else hi
        b0 = self.off + lo * self.esz
        b1 = self.off + hi * self.esz
        return [("pg", i) for i in range(b0 // PG, (b1 - 1) // PG + 1)]


class Carver:
    def __init__(self, arena, nbytes):
        self.arena = arena
        self.nbytes = nbytes
        self.off = 0

    base = 0

    def reset(self):
        self.off = self.base

    def take(self, n, dtype):
        esz = 4 if dtype == F32 else 2
        nb = (n * esz + PG - 1) // PG * PG
        assert self.off + nb <= self.nbytes, ("arena overflow", self.off, nb, self.nbytes)
        b = ABuf(self.arena, self.off, n, dtype)
        self.off += nb
        return b


def build(T=2048, NSEQ=2, stages=None, dump=None):
    NT = T // 128
    NB = T // 512
    ALL = ["ffn1_0", "mix_0", "xattn_0", "ffn2_0", "ffn1_1", "mix_1", "xattn_1", "ffn2_1", "final"]
    stages = ALL if stages is None else stages

    nc = bass.Bass("TRN2", target_bir_lowering=False)

    def din(name, shape):
        return nc.dram_tensor(name, list(shape), F32, kind="ExternalInput").ap()

    x_d = din("x", [NSEQ, T, D])
    mem_d = din("mem", [NSEQ, MEM, D])
    ffn_norm_d = [din("ffn1_norm", [2, D]), din("ffn2_norm", [2, D])]
    ffn_win_d = [din("ffn1_w_in", [2, D, 2 * FF]), din("ffn2_w_in", [2, D, 2 * FF])]
    ffn_wout_d = [din("ffn1_w_out", [2, FF, D]), din("ffn2_w_out", [2, FF, D])]
    mix_norm_d = din("mix_norm", [2, D])
    hg_win_d = din("hgrn_w_in", [1, D, 4 * D])
    hg_hn_d = din("hgrn_head_norm", [1, D])
    hg_wout_d = din("hgrn_w_out", [1, D, D])
    hg_lb_d = din("hgrn_lb_logits", [2, D])
    cv_win_d = din("conv_w_in", [1, D, 2 * D])
    cv_bin_d = din("conv_b_in", [1, 2 * D])
    cv_dw_d = din("conv_dw", [1, CW, D])
    cv_dwb_d = din("conv_dw_b", [1, D])
    cv_lng_d = din("conv_ln_g", [1, D])
    cv_lnb_d = din("conv_ln_b", [1, D])
    cv_wout_d = din("conv_w_out", [1, D, D])
    cv_bout_d = din("conv_b_out", [1, D])
    xa_norm_d = din("xattn_norm", [2, D])
    xa_wq_d = din("xattn_wq", [2, D, D])
    xa_wkv_d = din("xattn_wkv", [2, D, 2 * D])
    xa_wo_d = din("xattn_wo", [2, D, D])
    mem_norm_d = din("mem_norm", [D])
    fin_norm_d = din("final_norm", [D])
    y_d = nc.dram_tensor("y", [NSEQ, T, D], F32, kind="ExternalOutput").ap()
    dbg_d = nc.dram_tensor("dbg", [128, 8, 512], F32, kind="ExternalOutput").ap() if dump is not None else None

    HNT_B = KC * T * 2
    ARENA_B = HNT_B + 84 * 1024
    from contextlib import ExitStack
    es = ExitStack()
    with es:
        def sb(name, shape, dt):
            return es.enter_context(nc.sbuf_tensor(name, list(shape), dt))

        h = sb("h", [128, NT, D], F32)
        arena = sb("arena", [128, ARENA_B // 4], F32)
        gB = sb("gB", [128, 2, D], F32)
        hn_tok = sb("hn_tok", [128, 2, D], BF16)
        junk = sb("junk", [128, 2, D], BF16)
        ident = sb("ident", [128, 128], BF16)
        identf = sb("identf", [128, 128], F32)
        ones = sb("ones", [128, 128], BF16)
        onesf = sb("onesf", [128, 128], F32)
        maskT = sb("maskT", [128, 128], BF16)
        rmask = sb("rmask", [128, 512], F32)
        stat = sb("stat", [128, 3, 32], F32)
        vecs = sb("vecs", [128, 16, 8], F32)
        dwT = sb("dwT", [128, KC, CW], F32)
        memnT = sb("memnT", [128, KC, MEM], BF16)
        stf = sb("stf", [128, 128], F32)
        stb = sb("stb", [128, 128], BF16)
        sttmp = sb("sttmp", [128, 128], F32)
        ps = es.enter_context(nc.psum_tensor("ps", [128, 8, 512], F32))
        sems = {e: es.enter_context(nc.semaphore("s_" + e)) for e in ENGS}
        dma_sems = [es.enter_context(nc.semaphore("dq%d" % i)) for i in range(24)]
        block = es.enter_context(nc.Block())

        P = Prog()
        CV = Carver(arena, ARENA_B)
        CV.base = HNT_B
        hnT_buf = ABuf(arena, 0, KC * T, BF16)
        hnT = hnT_buf.ap.rearrange("p (c t) -> p c t", t=T)
        memt_buf = ABuf(arena, ARENA_B - 8192, 2 * D, F32)
        memt = memt_buf.ap.rearrange("p (t d) -> p t d", d=D)
        dwtok_buf = ABuf(arena, ARENA_B - 8192 - 4096, D, F32)
        dwtok = dwtok_buf.ap[0:CW, :]
        V_LB, V_OML, V_NOML, V_HN, V_BA, V_BG, V_DWB, V_LNG, V_LNB, V_L0, V_L1 = range(11)

        def psk(b):
            return [("ps", b)]

        def psb(b):
            return ps[:, b, :]

        def psb16(b):
            return ps[:, b, :].bitcast(BF16)

        for blk in range(NB):
            P.op("sp", lambda e, blk=blk: e.dma_start(out=h[:, blk * 4:(blk + 1) * 4, :], in_=x_d[0, blk * 512:(blk + 1) * 512, :].rearrange("(t p) d -> p t d", p=128)),
                 writes=[("h", blk * 4 + i) for i in range(4)], dma=True)
        P.op("pool", lambda e: e.memset(identf[:], 0.0), writes=["identf"])
        P.op("pool", lambda e: e.affine_select(out=identf[:], in_=identf[:], pattern=[[-1, 128]], compare_op=ALU.not_equal, fill=1.0, base=0, channel_multiplier=1), reads=["identf"], writes=["identf"])
        P.op("pool", lambda e: e.tensor_copy(out=ident[:], in_=identf[:]), reads=["identf"], writes=["ident"])
        P.op("pool", lambda e: e.memset(onesf[:], 1.0), writes=["onesf"])
        P.op("pool", lambda e: e.tensor_copy(out=ones[:], in_=onesf[:]), reads=["onesf"], writes=["ones"])
        P.op("pool", lambda e: e.affine_select(out=onesf[:], in_=onesf[:], pattern=[[1, 128]], compare_op=ALU.is_ge, fill=0.0, base=0, channel_multiplier=-1), reads=["onesf"], writes=["onesf"])
        P.op("pool", lambda e: e.tensor_copy(out=maskT[:], in_=onesf[:]), reads=["onesf"], writes=["maskT"])
        P.op("pool", lambda e: e.memset(rmask[:], 1.0), writes=["rmask"])
        P.op("pool", lambda e: e.memset(rmask[:].rearrange("p (c t) -> p c t", t=128)[:, :, 0:1], 0.0), reads=["rmask"], writes=["rmask"])

        P.op("pool", lambda e: e.memset(stf[:], 0.0), writes=["stf"])
        P.op("pool", lambda e: e.memset(stb[:], 0.0), writes=["stb"])
        def load_vec(slot, src):
            P.op("sp", lambda e: e.dma_start(out=vecs[:, slot, :], in_=src.rearrange("(c p) -> p c", p=128), allow_slow_non_contiguous=True), writes=[("vec", slot)], dma=True)
        load_vec(V_L0, hg_lb_d[0])
        load_vec(V_L1, hg_lb_d[1])
        load_vec(V_HN, hg_hn_d[0])
        load_vec(V_BA, cv_bin_d[0, 0:D])
        load_vec(V_BG, cv_bin_d[0, D:2 * D])
        load_vec(V_DWB, cv_dwb_d[0])
        load_vec(V_LNG, cv_lng_d[0])
        load_vec(V_LNB, cv_lnb_d[0])
        P.op("dve", lambda e: e.tensor_tensor(out=vecs[:, V_LB, :], in0=vecs[:, V_L0, :], in1=vecs[:, V_L1, :], op=ALU.subtract), reads=[("vec", V_L0), ("vec", V_L1)], writes=[("vec", V_LB)])
        P.op("act", lambda e: e.activation(out=vecs[:, V_LB, :], in_=vecs[:, V_LB, :], func=AF.Sigmoid), reads=[("vec", V_LB)], writes=[("vec", V_LB)])
        P.op("dve", lambda e: e.tensor_scalar(out=vecs[:, V_OML, :], in0=vecs[:, V_LB, :], scalar1=-1.0, scalar2=1.0, op0=ALU.mult, op1=ALU.add), reads=[("vec", V_LB)], writes=[("vec", V_OML)])
        P.op("dve", lambda e: e.tensor_scalar(out=vecs[:, V_NOML, :], in0=vecs[:, V_LB, :], scalar1=1.0, scalar2=-1.0, op0=ALU.mult, op1=ALU.add), reads=[("vec", V_LB)], writes=[("vec", V_NOML)])
        P.op("sp", lambda e: e.dma_start(out=dwtok[:], in_=cv_dw_d[0]), writes=dwtok_buf.keys(), dma=True)
        b0 = P.bank()
        for c in range(KC):
            P.op("pe", lambda e, c=c: e.transpose(out=ps[:, b0, c * 32:c * 32 + CW], in_=dwtok[:, c * 128:(c + 1) * 128], identity=identf[0:CW, 0:CW]), reads=dwtok_buf.keys() + ["identf"], writes=psk(b0))
        P.op("dve", lambda e: e.tensor_copy(out=dwT[:], in_=ps[:, b0, 0:256].rearrange("p (c j) -> p c j", j=32)[:, :, 0:CW]), reads=psk(b0), writes=["dwT"])

        def load_gain(slot, src):
            P.op("sp", lambda e: e.dma_start(out=gB[:, slot, :], in_=src.partition_broadcast(128)), writes=[("gB", slot)], dma=True)

        gslot = [0]

        def norm_stats(src_tile_ap, col, rkeys):
            js = col % 2
            P.op("act", lambda e: e.activation(out=junk[:, js, :], in_=src_tile_ap, func=AF.Square, accum_out=stat[:, 0, col:col + 1]), reads=rkeys, writes=[("ss", col), ("junk", js)])

        def norm_rstd(n):
            P.op("act", lambda e: e.activation(out=stat[:, 1, 0:n], in_=stat[:, 0, 0:n], func=AF.Sqrt, bias=RMS_EPS, scale=1.0 / D), reads=[("ss", c) for c in range(n)], writes=["sd"])
            P.op("dve", lambda e: e.reciprocal(out=stat[:, 2, 0:n], in_=stat[:, 1, 0:n]), reads=["sd"], writes=["rstd"])

        def begin_norm(gain_src):
            g = gslot[0] % 2
            gslot[0] += 1
            load_gain(g, gain_src)
            return g

        def block_rstd(blk):
            c0 = blk * 4
            for tt in range(c0, c0 + 4):
                norm_stats(h[:, tt, :], tt, [("h", tt)])
            P.op("act", lambda e: e.activation(out=stat[:, 1, c0:c0 + 4], in_=stat[:, 0, c0:c0 + 4], func=AF.Sqrt, bias=RMS_EPS, scale=1.0 / D), reads=[("ss", c) for c in range(c0, c0 + 4)], writes=[("sd", blk)])
            P.op("dve", lambda e: e.reciprocal(out=stat[:, 2, c0:c0 + 4], in_=stat[:, 1, c0:c0 + 4]), reads=[("sd", blk)], writes=[("rstd", blk)])

        def norm_block(blk, g, skip_rstd=False):
            if not skip_rstd:
                block_rstd(blk)
            for tt in range(blk * 4, blk * 4 + 4):
                s = tt % 2
                P.op("dve", lambda e, tt=tt, s=s: e.scalar_tensor_tensor(out=hn_tok[:, s, :], in0=h[:, tt, :], scalar=stat[:, 2, tt:tt + 1], in1=gB[:, g, :], op0=ALU.mult, op1=ALU.mult),
                     reads=[("h", tt), ("rstd", blk), ("gB", g)], writes=[("hn_tok", s)])
                b = P.bank()
                for c in range(KC):
                    P.op("pe", lambda e, c=c, s=s, b=b: e.transpose(out=psb16(b)[:, c * 128:(c + 1) * 128], in_=hn_tok[:, s, c * 128:(c + 1) * 128], identity=ident[:]),
                         reads=[("hn_tok", s), "ident"], writes=psk(b))
                eng = "act" if tt % 2 == 0 else "dve"
                if eng == "act":
                    fn = lambda e, tt=tt, b=b: e.activation(out=hnT[:, :, tt * 128:(tt + 1) * 128], in_=psb16(b).rearrange("p (c t) -> p c t", t=128), func=AF.Copy)
                else:
                    fn = lambda e, tt=tt, b=b: e.tensor_copy(out=hnT[:, :, tt * 128:(tt + 1) * 128], in_=psb16(b).rearrange("p (c t) -> p c t", t=128))
                P.op(eng, fn, reads=psk(b), writes=[k for c in range(KC) for k in hnT_buf.keys(c * T + tt * 128, c * T + (tt + 1) * 128)])

        fin_stage = [ABuf(arena, ARENA_B - 16384 + i * 4096, D, F32) for i in range(2)]
        fin_i = [0]

        def final_block(blk, g, sq_i):
            block_rstd(blk)
            for tt in range(blk * 4, blk * 4 + 4):
                fs = fin_stage[fin_i[0] % 2]
                fin_i[0] += 1
                P.op("dve", lambda e, tt=tt, fs=fs: e.scalar_tensor_tensor(out=fs.ap, in0=h[:, tt, :], scalar=stat[:, 2, tt:tt + 1], in1=gB[:, g, :], op0=ALU.mult, op1=ALU.mult),
                     reads=[("h", tt), ("rstd", blk), ("gB", g)], writes=fs.keys())
                P.op("sp", lambda e, tt=tt, fs=fs: e.dma_start(out=y_d[sq_i, tt * 128:(tt + 1) * 128, :], in_=fs.ap), reads=fs.keys(), writes=[("y", sq_i, tt)], dma=True)

        def hnT_keys(blk):
            if blk not in hk_cache:
                hk_cache[blk] = [k for c in range(KC) for k in hnT_buf.keys(c * T + blk * 512, c * T + (blk + 1) * 512)]
            return hk_cache[blk]
        hk_cache = {}

        def wload(dst_ap, src_ap, wkeys):
            P.op("pool", lambda e: e.dma_start(out=dst_ap, in_=src_ap), writes=wkeys, dma=True)

        def add_to_h(tt, half, b):
            P.op("dve", lambda e: e.tensor_tensor(out=h[:, tt, half * 512:(half + 1) * 512], in0=psb(b), in1=h[:, tt, half * 512:(half + 1) * 512], op=ALU.add),
                 reads=psk(b) + [("h", tt)], writes=[("h", tt)])

        def emit_ffn(which, layer, hook):
            win = ffn_win_d[which][layer]
            wout = ffn_wout_d[which][layer]
            CV.reset()
            NS = 3
            wi = [CV.take(KC * 512, BF16) for _ in range(NS)]
            wo = [CV.take(2 * D, BF16) for _ in range(NS)]
            sg = [CV.take(512, BF16) for _ in range(2)]
            act = [CV.take(2 * 512, BF16) for _ in range(3)]
            NG = FC // 2

            def load_group(g):
                s = g % NS
                wiv = wi[s].ap.rearrange("p (a k n) -> p a k n", a=2, n=256)
                f0 = g * 256
                wload(wiv[:, 0], win[:, f0:f0 + 256].rearrange("(k p) n -> p k n", p=128), wi[s].keys(0, KC * 256))
                wload(wiv[:, 1], win[:, FF + f0:FF + f0 + 256].rearrange("(k p) n -> p k n", p=128), wi[s].keys(KC * 256, 2 * KC * 256))
                wload(wo[s].ap.rearrange("p (j n) -> p j n", n=D), wout[f0:f0 + 256, :].rearrange("(j p) n -> p j n", p=128), wo[s].keys())

            load_group(0)
            load_group(1)
            ai = [0]
            pending = [None]

            def gateup(g, blk, a):
                s = g % NS
                wiv = wi[s].ap.rearrange("p (a k n) -> p a k n", a=2, n=256)
                actv = act[a].ap.rearrange("p (j n) -> p j n", n=512)
                for j in range(2):
                    bg = P.bank()
                    bu = P.bank()
                    for (bb, ga) in ((bg, 0), (bu, 1)):
                        for k in range(KC):
                            P.op("pe", lambda e, bb=bb, ga=ga, k=k, j=j: e.matmul(psb(bb), lhsT=wiv[:, ga, k, j * 128:(j + 1) * 128], rhs=hnT[:, k, blk * 512:(blk + 1) * 512], start=(k == 0), stop=(k == KC - 1)),
                                 reads=wi[s].keys(ga * KC * 256, (ga + 1) * KC * 256) + hnT_keys(blk), writes=psk(bb))
                    P.op("act", lambda e, bg=bg, j=j: e.activation(out=sg[j].ap, in_=psb(bg), func=AF.Silu), reads=psk(bg), writes=sg[j].keys())
                    P.op("dve", lambda e, bu=bu, j=j: e.tensor_tensor(out=actv[:, j, :], in0=psb(bu), in1=sg[j].ap, op=ALU.mult),
                         reads=psk(bu) + sg[j].keys(), writes=act[a].keys(j * 512, (j + 1) * 512))

            def outproj(g, blk, a):
                s = g % NS
                wov = wo[s].ap.rearrange("p (j n) -> p j n", n=D)
                actv = act[a].ap.rearrange("p (j n) -> p j n", n=512)
                for t4 in range(4):
                    tt = blk * 4 + t4
                    for half in range(2):
                        bo = P.bank()
                        for j in range(2):
                            P.op("pe", lambda e, bo=bo, j=j, t4=t4, half=half: e.matmul(psb(bo), lhsT=actv[:, j, t4 * 128:(t4 + 1) * 128], rhs=wov[:, j, half * 512:(half + 1) * 512], start=(j == 0), stop=(j == 1)),
                                 reads=act[a].keys() + wo[s].keys(), writes=psk(bo))
                        P.op("dve", lambda e, bo=bo, tt=tt, half=half: e.scalar_tensor_tensor(out=h[:, tt, half * 512:(half + 1) * 512], in0=psb(bo), scalar=0.5, in1=h[:, tt, half * 512:(half + 1) * 512], op0=ALU.mult, op1=ALU.add),
                             reads=psk(bo) + [("h", tt)], writes=[("h", tt)])

            for g in range(NG):
                for blk in range(NB):
                    a = ai[0] % 3
                    ai[0] += 1
                    gateup(g, blk, a)
                    if pending[0] is not None:
                        outproj(*pending[0])
                        if pending[0][0] == NG - 1:
                            hook(pending[0][1])
                    pending[0] = (g, blk, a)
                    if blk == 0 and g + 2 < NG:
                        load_group(g + 2)
            outproj(*pending[0])
            hook(pending[0][1])

        def emit_xattn(layer, hook):
            wq_d = xa_wq_d[layer]
            wkv_d = xa_wkv_d[layer]
            wo_d = xa_wo_d[layer]
            CV.reset()
            wq = CV.take(KC * D, BF16)
            wo = CV.take(KC * D, BF16)
            wkv = [CV.take(KC * 512, BF16) for _ in range(2)]
            kT = CV.take(KC * MEM, BF16)
            vsb = CV.take(2 * D, BF16)
            qT = CV.take(KC * 512, BF16)
            oT = CV.take(KC * 512, BF16)
            expT = [CV.take(2 * 512, BF16) for _ in range(2)]
            rB = [CV.take(512, F32) for _ in range(2)]
            wqv = wq.ap.rearrange("p (f k n) -> p f k n", f=2, n=512)
            wov = wo.ap.rearrange("p (f k n) -> p f k n", f=2, n=512)
            kTv = kT.ap.rearrange("p (c m) -> p c m", m=MEM)
            vv = vsb.ap.rearrange("p (t n) -> p t n", n=D)
            qTv = qT.ap.rearrange("p (c n) -> p c n", n=512)
            oTv = oT.ap.rearrange("p (c n) -> p c n", n=512)

            def load_kvblk(i):
                wload(wkv[i % 2].ap.rearrange("p (k n) -> p k n", n=512), wkv_d[:, i * 512:(i + 1) * 512].rearrange("(k p) n -> p k n", p=128), wkv[i % 2].keys())
            load_kvblk(0)
            load_kvblk(1)
            for hf in range(2):
                wload(wqv[:, hf], wq_d[:, hf * 512:(hf + 1) * 512].rearrange("(k p) n -> p k n", p=128), wq.keys(hf * KC * 512, (hf + 1) * KC * 512))
            for i in range(4):
                wv = wkv[i % 2].ap.rearrange("p (k n) -> p k n", n=512)
                if i < 2:
                    for c4 in range(4):
                        b = P.bank()
                        for k in range(KC):
                            P.op("pe", lambda e, b=b, k=k, c4=c4, wv=wv: e.matmul(ps[:, b, 0:MEM], lhsT=wv[:, k, c4 * 128:(c4 + 1) * 128], rhs=memnT[:, k, :], start=(k == 0), stop=(k == KC - 1)),
                                 reads=wkv[i % 2].keys() + ["memnT"], writes=psk(b))
                        P.op("act", lambda e, b=b, c=i * 4 + c4: e.activation(out=kTv[:, c, :], in_=ps[:, b, 0:MEM], func=AF.Copy, scale=float(256 ** -0.5)), reads=psk(b), writes=kT.keys())
                else:
                    for mt in range(2):
                        b = P.bank()
                        for k in range(KC):
                            P.op("pe", lambda e, b=b, k=k, mt=mt, wv=wv: e.matmul(psb(b), lhsT=memnT[:, k, mt * 128:(mt + 1) * 128], rhs=wv[:, k, :], start=(k == 0), stop=(k == KC - 1)),
                                 reads=wkv[i % 2].keys() + ["memnT"], writes=psk(b))
                        P.op("dve", lambda e, b=b, mt=mt, i=i: e.tensor_copy(out=vv[:, mt, (i - 2) * 512:(i - 1) * 512], in_=psb(b)), reads=psk(b), writes=vsb.keys())
                if i + 2 < 4:
                    load_kvblk(i + 2)
            for hf in range(2):
                wload(wov[:, hf], wo_d[:, hf * 512:(hf + 1) * 512].rearrange("(k p) n -> p k n", p=128), wo.keys(hf * KC * 512, (hf + 1) * KC * 512))
            xi = [0]

            def qproj(blk):
                for c in range(KC):
                    b = P.bank()
                    for k in range(KC):
                        P.op("pe", lambda e, b=b, k=k, c=c: e.matmul(psb(b), lhsT=wqv[:, c // 4, k, (c % 4) * 128:(c % 4 + 1) * 128], rhs=hnT[:, k, blk * 512:(blk + 1) * 512], start=(k == 0), stop=(k == KC - 1)),
                             reads=wq.keys() + hnT_keys(blk), writes=psk(b))
                    if c % 2 == 0:
                        P.op("act", lambda e, b=b, c=c: e.activation(out=qTv[:, c, :], in_=psb(b), func=AF.Copy), reads=psk(b), writes=qT.keys(c * 512, (c + 1) * 512))
                    else:
                        P.op("dve", lambda e, b=b, c=c: e.tensor_copy(out=qTv[:, c, :], in_=psb(b)), reads=psk(b), writes=qT.keys(c * 512, (c + 1) * 512))

            def woproj(blk):
                for t4 in range(4):
                    tt = blk * 4 + t4
                    for half in range(2):
                        b = P.bank()
                        for c in range(KC):
                            P.op("pe", lambda e, b=b, c=c, t4=t4, half=half: e.matmul(psb(b), lhsT=oTv[:, c, t4 * 128:(t4 + 1) * 128], rhs=wov[:, half, c, :], start=(c == 0), stop=(c == KC - 1)),
                                 reads=oT.keys() + wo.keys(), writes=psk(b))
                        add_to_h(tt, half, b)

            pend_hook = [None]
            qproj(0)
            for blk in range(NB):
                def scores(hh, xs):
                    ev = expT[xs].ap.rearrange("p (t n) -> p t n", n=512)
                    for mt in range(2):
                        b = P.bank()
                        for j in range(2):
                            c = 2 * hh + j
                            P.op("pe", lambda e, b=b, j=j, c=c, mt=mt: e.matmul(psb(b), lhsT=kTv[:, c, mt * 128:(mt + 1) * 128], rhs=qTv[:, c, :], start=(j == 0), stop=(j == 1)),
                                 reads=kT.keys() + qT.keys(c * 512, (c + 1) * 512), writes=psk(b))
                        P.op("act", lambda e, b=b, mt=mt, ev=ev: e.activation(out=ev[:, mt, :], in_=psb(b), func=AF.Exp), reads=psk(b), writes=expT[xs].keys(mt * 512, (mt + 1) * 512))

                def attend(hh, xs):
                    ev = expT[xs].ap.rearrange("p (t n) -> p t n", n=512)
                    bd = P.bank()
                    for mt in range(2):
                        P.op("pe", lambda e, bd=bd, mt=mt, ev=ev: e.matmul(psb(bd), lhsT=ones[:], rhs=ev[:, mt, :], start=(mt == 0), stop=(mt == 1)),
                             reads=expT[xs].keys() + ["ones"], writes=psk(bd))
                    P.op("dve", lambda e, bd=bd, xs=xs: e.reciprocal(out=rB[xs].ap, in_=psb(bd)), reads=psk(bd), writes=rB[xs].keys())
                    for j in range(2):
                        c = 2 * hh + j
                        b = P.bank()
                        for mt in range(2):
                            P.op("pe", lambda e, b=b, c=c, mt=mt, ev=ev: e.matmul(psb(b), lhsT=vv[:, mt, c * 128:(c + 1) * 128], rhs=ev[:, mt, :], start=(mt == 0), stop=(mt == 1)),
                                 reads=vsb.keys() + expT[xs].keys(), writes=psk(b))
                        P.op("dve", lambda e, b=b, c=c, xs=xs: e.tensor_tensor(out=oTv[:, c, :], in0=psb(b), in1=rB[xs].ap, op=ALU.mult), reads=psk(b) + rB[xs].keys(), writes=oT.keys(c * 512, (c + 1) * 512))

                scores(0, 0)
                for hh in range(4):
                    if hh + 1 < 4:
                        scores(hh + 1, (hh + 1) % 2)
                    attend(hh, hh % 2)
                if blk + 1 < NB:
                    qproj(blk + 1)
                if pend_hook[0] is not None:
                    hook(pend_hook[0])
                    pend_hook[0] = None
                woproj(blk)
                if hasattr(hook, "pre"):
                    hook.pre(blk)
                pend_hook[0] = blk
            hook(pend_hook[0])

        def emit_conv(hook):
            CV.reset()
            TP = T + PAD
            uT = CV.take(KC * TP, BF16)
            wout = CV.take(KC * D, BF16)
            mark = CV.off
            wi = [CV.take(KC * 256, BF16) for _ in range(3)]
            sg = [CV.take(512, BF16) for _ in range(2)]
            CV.off = mark
            dg = [CV.take(CW * 128, BF16) for _ in range(2)]
            u2b = [CV.take(512, BF16) for _ in range(2)]
            sq = [CV.take(512, BF16) for _ in range(2)]
            nrm = [CV.take(512, F32) for _ in range(2)]
            CH = Carver(arena, HNT_B) if HNT_B >= 31 * 1024 else CV
            u2 = CH.take(KC * 512, F32)
            yT = CH.take(KC * 512, BF16)
            meanB = CH.take(512, F32)
            rstdB = CH.take(512, F32)
            tmpB = CH.take(512, F32)
            uTv = uT.ap.rearrange("p (c t) -> p c t", t=TP)
            woutv = wout.ap.rearrange("p (f k n) -> p f k n", f=2, n=512)
            u2v = u2.ap.rearrange("p (c n) -> p c n", n=512)
            yTv = yT.ap.rearrange("p (c n) -> p c n", n=512)
            P.op("pool", lambda e: e.memset(uTv[:, :, 0:PAD], 0.0), writes=uT.keys())
            g = gslot[0] % 2
            gslot[0] += 1
            load_gain(g, cv_bout_d[0])
            def bout_add(tt):
                P.op("pool", lambda e: e.tensor_tensor(out=h[:, tt, :], in0=h[:, tt, :], in1=gB[:, g, :], op=ALU.add), reads=[("h", tt), ("gB", g)], writes=[("h", tt)])

            def load_wi(c):
                s = c % 3
                v = wi[s].ap.rearrange("p (a k n) -> p a k n", a=2, n=128)
                wload(v[:, 0], cv_win_d[0][:, c * 128:(c + 1) * 128].rearrange("(k p) n -> p k n", p=128), wi[s].keys(0, KC * 128))
                wload(v[:, 1], cv_win_d[0][:, D + c * 128:D + (c + 1) * 128].rearrange("(k p) n -> p k n", p=128), wi[s].keys(KC * 128, 2 * KC * 128))
            load_wi(0)
            load_wi(1)
            si = [0]
            for c in range(KC):
                if c + 2 < KC:
                    load_wi(c + 2)
                for tt in range(c * NT // KC, (c + 1) * NT // KC):
                    bout_add(tt)
                v = wi[c % 3].ap.rearrange("p (a k n) -> p a k n", a=2, n=128)
                for blk in range(NB):
                    ba = P.bank()
                    bg = P.bank()
                    for (bb, a) in ((ba, 0), (bg, 1)):
                        for k in range(KC):
                            P.op("pe", lambda e, bb=bb, a=a, k=k, v=v, blk=blk: e.matmul(psb(bb), lhsT=v[:, a, k, :], rhs=hnT[:, k, blk * 512:(blk + 1) * 512], start=(k == 0), stop=(k == KC - 1)),
                                 reads=wi[c % 3].keys() + hnT_keys(blk), writes=psk(bb))
                    s = si[0] % 2
                    si[0] += 1
                    P.op("act", lambda e, bg=bg, s=s, c=c: e.activation(out=sg[s].ap, in_=psb(bg), func=AF.Sigmoid, bias=vecs[:, V_BG, c:c + 1]), reads=psk(bg) + [("vec", V_BG)], writes=sg[s].keys())
                    lo = c * TP + PAD + blk * 512
                    P.op("dve", lambda e, ba=ba, s=s, c=c, blk=blk: e.scalar_tensor_tensor(out=uTv[:, c, PAD + blk * 512:PAD + (blk + 1) * 512], in0=psb(ba), scalar=vecs[:, V_BA, c:c + 1], in1=sg[s].ap, op0=ALU.add, op1=ALU.mult),
                         reads=psk(ba) + sg[s].keys() + [("vec", V_BA)], writes=uT.keys(lo, lo + 512))
            for hf in range(2):
                wload(woutv[:, hf], cv_wout_d[0][:, hf * 512:(hf + 1) * 512].rearrange("(k p) n -> p k n", p=128), wout.keys(hf * KC * 512, (hf + 1) * KC * 512))
            di = [0]

            def conv_outproj(blk):
                for t4 in range(4):
                    tt = blk * 4 + t4
                    for half in range(2):
                        b = P.bank()
                        for c in range(KC):
                            P.op("pe", lambda e, b=b, c=c, t4=t4, half=half: e.matmul(psb(b), lhsT=yTv[:, c, t4 * 128:(t4 + 1) * 128], rhs=woutv[:, half, c, :], start=(c == 0), stop=(c == KC - 1)),
                                 reads=yT.keys() + wout.keys(), writes=psk(b))
                        add_to_h(tt, half, b)
            pend_out = [None]

            def build_dg(c):
                d = di[0] % 2
                di[0] += 1
                dgv = dg[d].ap.rearrange("p (j n) -> p j n", n=128)
                NP_ = 21
                P.op("pool", lambda e: e.tensor_tensor(out=dgv[:, 0:NP_, :], in0=identf[:, None, :].to_broadcast([128, NP_, 128]), in1=dwT[:, c, 0:NP_, None].to_broadcast([128, NP_, 128]), op=ALU.mult),
                     reads=["identf", "dwT"], writes=dg[d].keys(0, NP_ * 128))
                P.op("dve", lambda e: e.tensor_tensor(out=dgv[:, NP_:CW, :], in0=identf[:, None, :].to_broadcast([128, CW - NP_, 128]), in1=dwT[:, c, NP_:CW, None].to_broadcast([128, CW - NP_, 128]), op=ALU.mult),
                     reads=["identf", "dwT"], writes=dg[d].keys(NP_ * 128, CW * 128))
                return d, dgv
            next_dg = [build_dg(0)]

            def normalize_chunk(c):
                n = c % 2
                P.op("dve", lambda e: e.tensor_tensor(out=nrm[n].ap, in0=u2v[:, c, :], in1=meanB.ap, op=ALU.subtract), reads=u2.keys(c * 512, (c + 1) * 512) + meanB.keys(), writes=nrm[n].keys())
                P.op("dve", lambda e: e.tensor_tensor(out=nrm[n].ap, in0=nrm[n].ap, in1=rstdB.ap, op=ALU.mult), reads=nrm[n].keys() + rstdB.keys(), writes=nrm[n].keys())
                P.op("dve", lambda e: e.tensor_scalar(out=nrm[n].ap, in0=nrm[n].ap, scalar1=vecs[:, V_LNG, c:c + 1], scalar2=vecs[:, V_LNB, c:c + 1], op0=ALU.mult, op1=ALU.add),
                     reads=nrm[n].keys() + [("vec", V_LNG), ("vec", V_LNB)], writes=nrm[n].keys())
                P.op("act", lambda e: e.activation(out=yTv[:, c, :], in_=nrm[n].ap, func=AF.Silu), reads=nrm[n].keys(), writes=yT.keys(c * 512, (c + 1) * 512))
            for blk in range(NB):
                bs = P.bank(pin=True)
                bq = P.bank(pin=True)
                pend_stats = [None]
                for c in range(KC):
                    d, dgv = next_dg[0]
                    if not (blk == NB - 1 and c == KC - 1):
                        next_dg[0] = build_dg((c + 1) % KC)
                    b = P.bank()
                    off = PAD - (CW - 1)
                    for j in range(CW):
                        lo = blk * 512 + j + off
                        P.op("pe", lambda e, b=b, j=j, c=c, lo=lo, dgv=dgv: e.matmul(psb(b), lhsT=dgv[:, j, :], rhs=uTv[:, c, lo:lo + 512], start=(j == 0), stop=(j == CW - 1)),
                             reads=dg[d].keys() + uT.keys(c * TP + lo, c * TP + lo + 512), writes=psk(b))
                    if pend_out[0] is not None:
                        normalize_chunk(c)
                    P.op("dve", lambda e, b=b, c=c: e.tensor_scalar(out=u2v[:, c, :], in0=psb(b), scalar1=vecs[:, V_DWB, c:c + 1], scalar2=None, op0=ALU.add), reads=psk(b) + [("vec", V_DWB)], writes=u2.keys(c * 512, (c + 1) * 512))
                    P.op("act", lambda e, c=c, d=d: e.activation(out=u2b[d].ap, in_=u2v[:, c, :], func=AF.Copy), reads=u2.keys(c * 512, (c + 1) * 512), writes=u2b[d].keys())
                    P.op("act", lambda e, c=c, d=d: e.activation(out=sq[d].ap, in_=u2v[:, c, :], func=AF.Square), reads=u2.keys(c * 512, (c + 1) * 512), writes=sq[d].keys())
                    if pend_out[0] is not None and c == KC - 1:
                        conv_outproj(pend_out[0])
                        pend_out[0] = None
                    def stats_mm(bs=bs, bq=bq, c=c, d=d):
                        P.op("pe", lambda e: e.matmul(psb(bs), lhsT=ones[:], rhs=u2b[d].ap, start=(c == 0), stop=(c == KC - 1)), reads=u2b[d].keys() + ["ones"], writes=psk(bs))
                        P.op("pe", lambda e: e.matmul(psb(bq), lhsT=ones[:], rhs=sq[d].ap, start=(c == 0), stop=(c == KC - 1)), reads=sq[d].keys() + ["ones"], writes=psk(bq))
                    if pend_stats[0] is not None:
                        pend_stats[0]()
                    pend_stats[0] = stats_mm
                pend_stats[0]()
                P.op("act", lambda e, bs=bs: e.activation(out=meanB.ap, in_=psb(bs), func=AF.Copy, scale=1.0 / D), reads=psk(bs), writes=meanB.keys())
                P.op("dve", lambda e: e.tensor_tensor(out=tmpB.ap, in0=meanB.ap, in1=meanB.ap, op=ALU.mult), reads=meanB.keys(), writes=tmpB.keys())
                P.op("dve", lambda e, bq=bq: e.scalar_tensor_tensor(out=tmpB.ap, in0=psb(bq), scalar=1.0 / D, in1=tmpB.ap, op0=ALU.mult, op1=ALU.subtract), reads=psk(bq) + tmpB.keys(), writes=tmpB.keys())
                P.unpin(bs)
                P.unpin(bq)
                P.op("act", lambda e: e.activation(out=tmpB.ap, in_=tmpB.ap, func=AF.Sqrt, bias=LN_EPS), reads=tmpB.keys(), writes=tmpB.keys())
                P.op("dve", lambda e: e.reciprocal(out=rstdB.ap, in_=tmpB.ap), reads=tmpB.keys(), writes=rstdB.keys())
                pend_out[0] = blk
            if hasattr(hook, "pre"):
                for blk in range(NB - 1):
                    hook.pre(blk)
            for c in range(KC):
                normalize_chunk(c)
            conv_outproj(pend_out[0])
            if hasattr(hook, "pre"):
                hook.pre(NB - 1)
            for blk in range(NB):
                hook(blk)

        def emit_hgrn(hook):
            CV.reset()
            wi = [CV.take(KC * 512, BF16) for _ in range(2)]
            wo = [CV.take(D, BF16) for _ in range(3)]
            NSL = 2
            SQ = [CV.take(512, F32) for _ in range(NSL)]
            SG = [CV.take(512, F32) for _ in range(NSL)]
            A = [CV.take(512, F32) for _ in range(NSL)]
            Bf = [CV.take(512, F32) for _ in range(NSL)]
            Cf = [CV.take(512, F32) for _ in range(NSL)]
            qs = [CV.take(512, BF16) for _ in range(NSL)]
            gs = [CV.take(512, BF16) for _ in range(NSL)]
            qg = [CV.take(512, BF16) for _ in range(NSL)]
            kg = [CV.take(512, BF16) for _ in range(NSL)]
            vt = [CV.take(512, BF16) for _ in range(NSL)]
            scT = [CV.take(512, BF16) for _ in range(NSL)]
            kgt = [CV.take(512, BF16) for _ in range(NSL)]
            dSe = [CV.take(512, F32) for _ in range(NSL)]
            stF = [CV.take(512, F32) for _ in range(NSL)]
            stB = [CV.take(512, BF16) for _ in range(NSL)]
            sqo = [CV.take(512, BF16) for _ in range(2)]
            rsB = [CV.take(512, F32) for _ in range(2)]
            og = [CV.take(512, F32) for _ in range(2)]
            ogT = [CV.take(512, BF16) for _ in range(2)]

            def load_head(hd):
                s = hd % 2
                v = wi[s].ap.rearrange("p (a k n) -> p a k n", a=4, n=128)
                for a in range(4):
                    wload(v[:, a], hg_win_d[0][:, a * D + hd * 128:a * D + (hd + 1) * 128].rearrange("(k p) n -> p k n", p=128), wi[s].keys(a * KC * 128, (a + 1) * KC * 128))
                wload(wo[hd % 3].ap, hg_wout_d[0][hd * 128:(hd + 1) * 128, :], wo[hd % 3].keys())

            def proj(hd, blk, u):
                s = hd % 2
                v = wi[s].ap.rearrange("p (a k n) -> p a k n", a=4, n=128)
                lbc = vecs[:, V_LB, hd:hd + 1]
                omlc = vecs[:, V_OML, hd:hd + 1]
                nomlc = vecs[:, V_NOML, hd:hd + 1]

                def grp(a):
                    bb = P.bank()
                    for k in range(KC):
                        P.op("pe", lambda e, bb=bb, a=a, k=k: e.matmul(psb(bb), lhsT=v[:, a, k, :], rhs=hnT[:, k, blk * 512:(blk + 1) * 512], start=(k == 0), stop=(k == KC - 1)),
                             reads=wi[s].keys(a * KC * 128, (a + 1) * KC * 128) + hnT_keys(blk), writes=psk(bb))
                    return bb
                bf = grp(1)
                bq = grp(0)
                bgt = grp(3)
                P.op("act", lambda e: e.activation(out=A[u].ap, in_=psb(bf), func=AF.Sigmoid), reads=psk(bf), writes=A[u].keys())
                P.op("act", lambda e: e.activation(out=SQ[u].ap, in_=psb(bq), func=AF.Sigmoid), reads=psk(bq), writes=SQ[u].keys())
                P.op("act", lambda e: e.activation(out=SG[u].ap, in_=psb(bgt), func=AF.Sigmoid), reads=psk(bgt), writes=SG[u].keys())
                P.op("act", lambda e: e.activation(out=Bf[u].ap, in_=A[u].ap, func=AF.Ln, scale=omlc, bias=lbc), reads=A[u].keys() + [("vec", V_LB), ("vec", V_OML)], writes=Bf[u].keys())
                P.op("dve", lambda e: e.tensor_tensor(out=SQ[u].ap, in0=psb(bq), in1=SQ[u].ap, op=ALU.mult), reads=psk(bq) + SQ[u].keys(), writes=SQ[u].keys())
                P.op("dve", lambda e: e.tensor_tensor(out=gs[u].ap, in0=psb(bgt), in1=SG[u].ap, op=ALU.mult), reads=psk(bgt) + SG[u].keys(), writes=gs[u].keys())
                P.op("pool", lambda e: e.tensor_scalar(out=A[u].ap, in0=A[u].ap, scalar1=nomlc, scalar2=omlc, op0=ALU.mult, op1=ALU.add), reads=A[u].keys() + [("vec", V_NOML), ("vec", V_OML)], writes=A[u].keys())
                P.op("dve", lambda e: e.tensor_tensor_scan(out=Cf[u].ap, data0=rmask[:], data1=Bf[u].ap, initial=0.0, op0=ALU.mult, op1=ALU.add), reads=Bf[u].keys() + ["rmask"], writes=Cf[u].keys())
                P.op("act", lambda e: e.activation(out=Bf[u].ap, in_=Cf[u].ap, func=AF.Exp), reads=Cf[u].keys(), writes=Bf[u].keys())
                P.op("act", lambda e: e.activation(out=Cf[u].ap, in_=Cf[u].ap, func=AF.Exp, scale=-1.0), reads=Cf[u].keys(), writes=Cf[u].keys())
                P.op("pool", lambda e: e.tensor_tensor(out=qg[u].ap, in0=SQ[u].ap, in1=Bf[u].ap, op=ALU.mult), reads=SQ[u].keys() + Bf[u].keys(), writes=qg[u].keys())
                P.op("pool", lambda e: e.tensor_tensor(out=kg[u].ap, in0=A[u].ap, in1=Cf[u].ap, op=ALU.mult), reads=A[u].keys() + Cf[u].keys(), writes=kg[u].keys())
                yield
                bv = P.bank()
                for t4 in range(4):
                    for k in range(KC):
                        P.op("pe", lambda e, t4=t4, k=k: e.matmul(ps[:, bv, t4 * 128:(t4 + 1) * 128], lhsT=hnT[:, k, blk * 512 + t4 * 128:blk * 512 + (t4 + 1) * 128], rhs=v[:, 2, k, :], start=(k == 0), stop=(k == KC - 1)),
                             reads=wi[s].keys(2 * KC * 128, 3 * KC * 128) + hnT_keys(blk), writes=psk(bv))
                P.op("dve", lambda e: e.tensor_copy(out=vt[u].ap, in_=psb(bv)), reads=psk(bv), writes=vt[u].keys())
                if dump is not None and hd == dump[0] and blk == dump[1]:
                    for di_, bufd in enumerate((A[u], Bf[u], Cf[u], qg[u], kg[u], vt[u], gs[u])):
                        P.op("pool", lambda e, di_=di_, bufd=bufd: e.dma_start(out=dbg_d[:, di_, :], in_=bufd.ap), reads=bufd.keys(), writes=[("dbg", di_)], dma=True)
                yield

            ri = [0]

            def rec(hd, blk, u):
                s = hd % 2
                vtv = vt[u].ap.rearrange("p (t n) -> p t n", n=128)
                scTv = scT[u].ap.rearrange("p (t n) -> p t n", n=128)
                dSev = dSe[u].ap.rearrange("p (t n) -> p t n", n=128)
                stFv = stF[u].ap.rearrange("p (t n) -> p t n", n=128)
                stBv = stB[u].ap.rearrange("p (t n) -> p t n", n=128)
                if blk == 0:
                    carry_f, carry_b, carry_k = stf[:], stb[:], ["stf", "stb"]
                else:
                    pf = stF[1 - u].ap.rearrange("p (t n) -> p t n", n=128)
                    pb = stB[1 - u].ap.rearrange("p (t n) -> p t n", n=128)
                    carry_f, carry_b, carry_k = pf[:, 3, :], pb[:, 3, :], stF[1 - u].keys() + stB[1 - u].keys()
                bsc = P.bank()
                for c in range(4):
                    cs = slice(c * 128, (c + 1) * 128)
                    P.op("pe", lambda e, cs=cs: e.matmul(ps[:, bsc, cs], lhsT=kg[u].ap[:, cs], rhs=qg[u].ap[:, cs], start=True, stop=True), reads=kg[u].keys() + qg[u].keys(), writes=psk(bsc))
                bkt = P.bank()
                for c in range(4):
                    cs = slice(c * 128, (c + 1) * 128)
                    P.op("pe", lambda e, cs=cs: e.transpose(out=psb16(bkt)[:, cs], in_=kg[u].ap[:, cs], identity=ident[:]), reads=kg[u].keys() + ["ident"], writes=psk(bkt))
                P.op("dve", lambda e: e.tensor_tensor(out=scTv, in0=ps[:, bsc, :].rearrange("p (t n) -> p t n", n=128), in1=maskT[:, None, :].to_broadcast([128, 4, 128]), op=ALU.mult), reads=psk(bsc) + ["maskT"], writes=scT[u].keys())
                P.op("act", lambda e: e.activation(out=kgt[u].ap, in_=psb16(bkt)[:, 0:512], func=AF.Copy), reads=psk(bkt), writes=kgt[u].keys())
                yield
                bds = P.bank()
                for c in range(4):
                    cs = slice(c * 128, (c + 1) * 128)
                    P.op("pe", lambda e, c=c, cs=cs: e.matmul(ps[:, bds, cs], lhsT=kgt[u].ap[:, cs], rhs=vtv[:, c, :], start=True, stop=True), reads=kgt[u].keys() + vt[u].keys(), writes=psk(bds))
                egls = [Bf[u].ap[:, c * 128 + 127:c * 128 + 128] for c in range(4)]
                for c in range(4):
                    cs = slice(c * 128, (c + 1) * 128)
                    P.op("dve", lambda e, c=c, cs=cs: e.tensor_scalar(out=dSev[:, c, :], in0=ps[:, bds, cs], scalar1=egls[c], scalar2=None, op0=ALU.mult), reads=psk(bds) + Bf[u].keys(), writes=dSe[u].keys(c * 128, (c + 1) * 128))
                for c in range(4):
                    prev = carry_f if c == 0 else stFv[:, c - 1, :]
                    pk = carry_k if c == 0 else stF[u].keys((c - 1) * 128, c * 128)
                    P.op("dve", lambda e, c=c, prev=prev: e.scalar_tensor_tensor(out=stFv[:, c, :], in0=prev, scalar=egls[c], in1=dSev[:, c, :], op0=ALU.mult, op1=ALU.add),
                         reads=pk + Bf[u].keys() + dSe[u].keys(c * 128, (c + 1) * 128), writes=stF[u].keys(c * 128, (c + 1) * 128))
                    P.op("dve", lambda e, c=c, prev=prev: e.scalar_tensor_tensor(out=stBv[:, c, :], in0=prev, scalar=egls[c], in1=dSev[:, c, :], op0=ALU.mult, op1=ALU.add),
                         reads=pk + Bf[u].keys() + dSe[u].keys(c * 128, (c + 1) * 128), writes=stB[u].keys(c * 128, (c + 1) * 128))
                yield
                bo = P.bank()
                for c in range(4):
                    cs = slice(c * 128, (c + 1) * 128)
                    P.op("pe", lambda e, c=c, cs=cs: e.matmul(ps[:, bo, cs], lhsT=vtv[:, c, :], rhs=scTv[:, c, :], start=True, stop=False), reads=vt[u].keys() + scT[u].keys(), writes=psk(bo))
                    sb_prev = carry_b if c == 0 else stBv[:, c - 1, :]
                    sk = carry_k if c == 0 else stB[u].keys((c - 1) * 128, c * 128)
                    P.op("pe", lambda e, cs=cs, sb_prev=sb_prev: e.matmul(ps[:, bo, cs], lhsT=sb_prev, rhs=qg[u].ap[:, cs], start=False, stop=True), reads=sk + qg[u].keys(), writes=psk(bo))
                o = ri[0] % 2
                ri[0] += 1
                P.op("act", lambda e: e.activation(out=sqo[o].ap, in_=psb(bo), func=AF.Square), reads=psk(bo), writes=sqo[o].keys())
                bs = P.bank()
                P.op("pe", lambda e: e.matmul(psb(bs), lhsT=ones[:], rhs=sqo[o].ap, start=True, stop=True), reads=sqo[o].keys() + ["ones"], writes=psk(bs))
                P.op("act", lambda e: e.activation(out=rsB[o].ap, in_=psb(bs), func=AF.Ln, bias=RMS_EPS, scale=1.0 / 128), reads=psk(bs), writes=rsB[o].keys())
                P.op("act", lambda e: e.activation(out=rsB[o].ap, in_=rsB[o].ap, func=AF.Exp, scale=-0.5), reads=rsB[o].keys(), writes=rsB[o].keys())
                P.op("dve", lambda e: e.tensor_tensor(out=og[o].ap, in0=psb(bo), in1=rsB[o].ap, op=ALU.mult), reads=psk(bo) + rsB[o].keys(), writes=og[o].keys())
                P.op("dve", lambda e: e.scalar_tensor_tensor(out=ogT[o].ap, in0=og[o].ap, scalar=vecs[:, V_HN, hd:hd + 1], in1=gs[u].ap, op0=ALU.mult, op1=ALU.mult), reads=og[o].keys() + gs[u].keys() + [("vec", V_HN)], writes=ogT[o].keys())
                rec_o[(hd, blk)] = o
                yield

            rec_o = {}

            def outp(hd, blk):
                o = rec_o[(hd, blk)]
                w = wo[hd % 3]
                for t4 in range(4):
                    tt = blk * 4 + t4
                    for half in range(2):
                        b = P.bank()
                        P.op("pe", lambda e, b=b, t4=t4, half=half: e.matmul(psb(b), lhsT=ogT[o].ap[:, t4 * 128:(t4 + 1) * 128], rhs=w.ap[:, half * 512:(half + 1) * 512], start=True, stop=True),
                             reads=ogT[o].keys() + w.keys(), writes=psk(b))
                        add_to_h(tt, half, b)
                    yield
                if hd == KC - 1:
                    hook(blk)

            def interleave(gens):
                alive = [g for g in gens if g is not None]
                while alive:
                    for gen in list(alive):
                        try:
                            next(gen)
                        except StopIteration:
                            alive.remove(gen)

            units = [(hd, blk) for hd in range(KC) for blk in range(NB)]
            load_head(0)
            def step(gen):
                if gen is not None:
                    try:
                        next(gen)
                    except StopIteration:
                        pass

            def drain(gen):
                if gen is not None:
                    for _ in gen:
                        pass
            n = len(units)
            for i in range(n + 2):
                gp = proj(units[i][0], units[i][1], i % NSL) if i < n else None
                gr = rec(units[i - 1][0], units[i - 1][1], (i - 1) % NSL) if 1 <= i <= n else None
                go = outp(*units[i - 2]) if 2 <= i <= n + 1 else None
                step(gr)
                step(gr)
                step(gp)
                step(go)
                step(go)
                step(gp)
                drain(go)
                drain(gr)
                drain(gp)
                if i < n and units[i][1] == 0 and units[i][0] + 1 < KC:
                    load_head(units[i][0] + 1)

        for sq_i in range(NSEQ):
            if any(st.startswith("xattn") for st in stages):
                g = gslot[0] % 2
                gslot[0] += 1
                load_gain(g, mem_norm_d)
                P.op("sp", lambda e, sq_i=sq_i: e.dma_start(out=memt[:], in_=mem_d[sq_i].rearrange("(t p) d -> p t d", p=128)), writes=memt_buf.keys(), dma=True)
                for mt in range(2):
                    norm_stats(memt[:, mt, :], 16 + mt, memt_buf.keys())
                P.op("act", lambda e: e.activation(out=stat[:, 1, 16:18], in_=stat[:, 0, 16:18], func=AF.Sqrt, bias=RMS_EPS, scale=1.0 / D), reads=[("ss", 16), ("ss", 17)], writes=[("sd", "m")])
                P.op("dve", lambda e: e.reciprocal(out=stat[:, 2, 16:18], in_=stat[:, 1, 16:18]), reads=[("sd", "m")], writes=[("rstd", "m")])
                for mt in range(2):
                    s = mt % 2
                    P.op("dve", lambda e, mt=mt, s=s, g=g: e.scalar_tensor_tensor(out=hn_tok[:, s, :], in0=memt[:, mt, :], scalar=stat[:, 2, 16 + mt:17 + mt], in1=gB[:, g, :], op0=ALU.mult, op1=ALU.mult),
                         reads=memt_buf.keys() + [("rstd", "m"), ("gB", g)], writes=[("hn_tok", s)])
                    b = P.bank()
                    for c in range(KC):
                        P.op("pe", lambda e, c=c, s=s, b=b: e.transpose(out=psb16(b)[:, c * 128:(c + 1) * 128], in_=hn_tok[:, s, c * 128:(c + 1) * 128], identity=ident[:]),
                             reads=[("hn_tok", s), "ident"], writes=psk(b))
                    P.op("dve", lambda e, mt=mt, b=b: e.tensor_copy(out=memnT[:, :, mt * 128:(mt + 1) * 128], in_=psb16(b).rearrange("p (c t) -> p c t", t=128)), reads=psk(b), writes=["memnT"])
            seq_stages = []
            for layer in range(2):
                if "ffn1_%d" % layer in stages:
                    seq_stages.append((ffn_norm_d[0][layer], lambda hook, layer=layer: emit_ffn(0, layer, hook)))
                if "mix_%d" % layer in stages:
                    seq_stages.append((mix_norm_d[layer], (lambda hook: emit_hgrn(hook)) if layer == 0 else (lambda hook: emit_conv(hook))))
                if "xattn_%d" % layer in stages:
                    seq_stages.append((xa_norm_d[layer], lambda hook, layer=layer: emit_xattn(layer, hook)))
                if "ffn2_%d" % layer in stages:
                    seq_stages.append((ffn_norm_d[1][layer], lambda hook, layer=layer: emit_ffn(1, layer, hook)))
            has_final = "final" in stages

            def store_block(blk, sq_i=sq_i):
                for tt in range(blk * 4, blk * 4 + 4):
                    P.op("sp", lambda e, tt=tt: e.dma_start(out=y_d[sq_i, tt * 128:(tt + 1) * 128, :], in_=h[:, tt, :]), reads=[("h", tt)], writes=[("y", sq_i, tt)], dma=True)

            def load_x_block(sq, blk):
                P.op("sp", lambda e: e.dma_start(out=h[:, blk * 4:(blk + 1) * 4, :], in_=x_d[sq, blk * 512:(blk + 1) * 512, :].rearrange("(t p) d -> p t d", p=128)),
                     writes=[("h", blk * 4 + i) for i in range(4)], dma=True)

            def tail_hook_factory(sq_i=sq_i):
                if has_final:
                    gf = begin_norm(fin_norm_d)
                    inner = lambda blk: final_block(blk, gf, sq_i)
                else:
                    inner = store_block

                def tail(blk):
                    inner(blk)
                    if sq_i + 1 < NSEQ:
                        load_x_block(sq_i + 1, blk)
                return tail

            if not seq_stages:
                hk = tail_hook_factory()
                for blk in range(NB):
                    hk(blk)
            else:
                g0 = begin_norm(seq_stages[0][0])
                for blk in range(NB):
                    norm_block(blk, g0)
                for i, (gsrc, emit_fn) in enumerate(seq_stages):
                    if i + 1 < len(seq_stages):
                        gn = begin_norm(seq_stages[i + 1][0])
                        pre_done = set()

                        def hk(blk, gn=gn, pre_done=pre_done):
                            norm_block(blk, gn, skip_rstd=(blk in pre_done))

                        def hk_pre(blk, pre_done=pre_done):
                            block_rstd(blk)
                            pre_done.add(blk)
                        hk.pre = hk_pre
                    else:
                        hk = tail_hook_factory()
                    emit_fn(hk)

        run = P.emit(sems, dma_sems)

        @block.sync
        def _(e):
            run("sp", e)

        @block.scalar
        def _(e):
            run("act", e)

        @block.vector
        def _(e):
            run("dve", e)

        @block.gpsimd
        def _(e):
            run("pool", e)

        @block.tensor
        def _(e):
            run("pe", e)
        build.stats = P.stats
    return nc


_IN_NAMES = ["x", "mem", "ffn1_norm", "ffn1_w_in", "ffn1_w_out", "mix_norm", "hgrn_w_in", "hgrn_head_norm",
             "hgrn_w_out", "hgrn_lb_logits", "conv_w_in", "conv_b_in", "conv_dw", "conv_dw_b", "conv_ln_g",
             "conv_ln_b", "conv_w_out", "conv_b_out", "xattn_norm", "xattn_wq", "xattn_wkv", "xattn_wo",
             "ffn2_norm", "ffn2_w_in", "ffn2_w_out", "mem_norm", "final_norm"]


def kernel(**inputs):
    arrs = {k: np.ascontiguousarray(np.asarray(inputs[k], dtype=np.float32)) for k in _IN_NAMES}
    B, T, _ = arrs["x"].shape
    nseq = B // N_CORES
    nc = build(T=T, NSEQ=nseq)
    in_maps = []
    for c in range(N_CORES):
        m = dict(arrs)
        m["x"] = np.ascontiguousarray(arrs["x"][c * nseq:(c + 1) * nseq])
        m["mem"] = np.ascontiguousarray(arrs["mem"][c * nseq:(c + 1) * nseq])
        in_maps.append(m)
    res = run_bass_kernel_spmd(nc, in_maps, core_ids=list(range(N_CORES)))
    return np.concatenate([r["y"] for r in res.results], axis=0).astype(np.float32)
```
